# Optimizing a Trainium2 kernel written in Bass

```python
import math
import jax, jax.numpy as jnp
from jax import lax
import numpy as np

D_MODEL = 2048
BATCH = 4
SEQ = 4096
DEPTH = 1

CONV_CH = D_MODEL
CONV_WIDTH = 31

SSM_EXPAND = 2
D_INNER = SSM_EXPAND * D_MODEL
HEADDIM = 64
N_HEADS = D_INNER // HEADDIM
D_STATE = 128
N_GROUPS = 8
SSM_CONV = 4
CHUNK = 128
XBC_DIM = D_INNER + 2 * N_GROUPS * D_STATE

NORM_EPS = 1e-6
LN_EPS = 1e-5
DT_MIN = 0.001
DT_MAX = 0.1

IN_SIZES = (CONV_CH, CONV_CH, CONV_CH,
            D_INNER, XBC_DIM, N_HEADS,
            D_MODEL, D_MODEL)
D_IN_PROJ = sum(IN_SIZES)
SPLITS = tuple(int(v) for v in np.cumsum(IN_SIZES)[:-1])

kernel_name = "hybrid_conformer_ssd_gated_block"


def rmsnorm(x, w, eps=NORM_EPS):
    xf = x.astype(jnp.float32)
    y = xf * lax.rsqrt(jnp.mean(xf * xf, axis=-1, keepdims=True) + eps)
    return y.astype(x.dtype) * w


def causal_depthwise_conv(u, w, b):
    k = w.shape[0]
    out = lax.conv_general_dilated(
        u, w[:, None, :].astype(u.dtype), window_strides=(1,), padding=[(k - 1, 0)],
        dimension_numbers=("NWC", "WIO", "NWC"), feature_group_count=u.shape[-1])
    return out + b


def conformer_conv_branch(a_val, a_gate, a_z, conv_w, conv_b, ln_w, ln_b, w_out):
    u = a_val * jax.nn.sigmoid(a_gate)
    u = causal_depthwise_conv(u, conv_w, conv_b)
    uf = u.astype(jnp.float32)
    mu = jnp.mean(uf, axis=-1, keepdims=True)
    var = jnp.mean(jnp.square(uf - mu), axis=-1, keepdims=True)
    u = ((uf - mu) * lax.rsqrt(var + LN_EPS)).astype(u.dtype) * ln_w + ln_b
    u = jax.nn.silu(u) * jax.nn.silu(a_z)
    return u @ w_out


def ssd_chunked(xs, dt, a, bmat, cmat):
    b, s, h, p = xs.shape
    g, n = bmat.shape[-2:]
    r = h // g
    nc = s // CHUNK

    def to_chunks(t):
        return jnp.moveaxis(t.reshape(b, nc, CHUNK, *t.shape[2:]), 1, 0)

    xg = to_chunks(xs.reshape(b, s, g, r, p))
    dtg = to_chunks(dt.reshape(b, s, g, r))
    dag = dtg * a.reshape(g, r)
    bc_all = to_chunks(bmat)
    cc_all = to_chunks(cmat)
    causal = jnp.tril(jnp.ones((CHUNK, CHUNK), dtype=bool))[None, :, :, None, None]

    def step(state, inp):
        xc, dtc, dac, bc, cc = inp
        cum = jnp.cumsum(dac, axis=1)
        seg = cum[:, :, None] - cum[:, None]
        decay = jnp.exp(jnp.where(causal, seg, -jnp.inf))
        cb = jnp.einsum("blgn,bsgn->blsg", cc, bc)
        m = cb[..., None] * decay * dtc[:, None]
        y_diag = jnp.einsum("blsgr,bsgrp->blgrp", m, xc)
        y_off = jnp.einsum("blgn,bgrpn->blgrp", cc, state) * jnp.exp(cum)[..., None]
        to_end = jnp.exp(cum[:, -1:] - cum) * dtc
        new_state = (state * jnp.exp(cum[:, -1])[..., None, None]
                     + jnp.einsum("bsgn,bsgr,bsgrp->bgrpn", bc, to_end, xc))
        return new_state, y_diag + y_off

    init = jnp.zeros((b, g, r, p, n), jnp.float32)
    _, y = lax.scan(step, init, (xg, dtg, dag, bc_all, cc_all))
    return jnp.moveaxis(y, 0, 1).reshape(b, s, h, p)


def mamba2_branch(s_z, s_xbc, s_dt, conv_w, conv_b, dt_bias, a_log, d_skip, norm_w, w_out):
    b, s, _ = s_xbc.shape
    xbc = jax.nn.silu(causal_depthwise_conv(s_xbc, conv_w, conv_b))
    xs, bm, cm = jnp.split(xbc, [D_INNER, D_INNER + N_GROUPS * D_STATE], axis=-1)
    xs = xs.reshape(b, s, N_HEADS, HEADDIM).astype(jnp.float32)
    bm = bm.reshape(b, s, N_GROUPS, D_STATE).astype(jnp.float32)
    cm = cm.reshape(b, s, N_GROUPS, D_STATE).astype(jnp.float32)
    dt = jax.nn.softplus(s_dt.astype(jnp.float32) + dt_bias.astype(jnp.float32))
    a = -jnp.exp(a_log.astype(jnp.float32))
    y = ssd_chunked(xs, dt, a, bm, cm)
    y = y + d_skip.astype(jnp.float32)[:, None] * xs
    y = y.reshape(b, s, D_INNER) * jax.nn.silu(s_z.astype(jnp.float32))
    yg = y.reshape(b, s, N_GROUPS, D_INNER // N_GROUPS)
    yg = yg * lax.rsqrt(jnp.mean(yg * yg, axis=-1, keepdims=True) + NORM_EPS)
    y = yg.reshape(b, s, D_INNER).astype(s_z.dtype) * norm_w
    return y @ w_out


def setup_inputs(seed: int = 0) -> dict:
    key = jax.random.key(seed)
    ks = jax.random.split(key, 17)
    f32 = jnp.float32
    L = DEPTH

    def nrm(k, shape, scale):
        return jax.random.normal(k, shape, f32) * scale

    x = jax.random.normal(ks[0], (BATCH, SEQ, D_MODEL), f32)
    norm_w = 1.0 + nrm(ks[1], (L, D_MODEL), 0.02)
    w_in = nrm(ks[2], (L, D_MODEL, D_IN_PROJ), D_MODEL ** -0.5)
    conv_a_w = nrm(ks[3], (L, CONV_WIDTH, CONV_CH), CONV_WIDTH ** -0.5)
    conv_a_b = nrm(ks[4], (L, CONV_CH), 0.02)
    ln_a_w = 1.0 + nrm(ks[5], (L, CONV_CH), 0.02)
    ln_a_b = nrm(ks[6], (L, CONV_CH), 0.02)
    w_a_out = nrm(ks[7], (L, CONV_CH, D_MODEL), CONV_CH ** -0.5)
    conv_b_w = nrm(ks[8], (L, SSM_CONV, XBC_DIM), SSM_CONV ** -0.5)
    conv_b_b = nrm(ks[9], (L, XBC_DIM), 0.02)
    u = jax.random.uniform(ks[10], (L, N_HEADS), f32)
    dt0 = jnp.exp(u * (math.log(DT_MAX) - math.log(DT_MIN)) + math.log(DT_MIN))
    dt_bias = dt0 + jnp.log(-jnp.expm1(-dt0))
    a_log = jnp.log(jax.random.uniform(ks[11], (L, N_HEADS), f32, 1.0, 16.0))
    d_skip = 1.0 + nrm(ks[12], (L, N_HEADS), 0.1)
    ssm_norm_w = 1.0 + nrm(ks[13], (L, D_INNER), 0.02)
    w_b_out = nrm(ks[14], (L, D_INNER, D_MODEL), D_INNER ** -0.5)
    w_o = nrm(ks[15], (L, D_MODEL, D_MODEL), D_MODEL ** -0.5)
    final_norm_w = 1.0 + nrm(ks[16], (D_MODEL,), 0.02)
    return {"x": x, "norm_w": norm_w, "w_in": w_in,
            "conv_a_w": conv_a_w, "conv_a_b": conv_a_b, "ln_a_w": ln_a_w, "ln_a_b": ln_a_b,
            "w_a_out": w_a_out, "conv_b_w": conv_b_w, "conv_b_b": conv_b_b,
            "dt_bias": dt_bias, "a_log": a_log, "d_skip": d_skip, "ssm_norm_w": ssm_norm_w,
            "w_b_out": w_b_out, "w_o": w_o, "final_norm_w": final_norm_w}


def reference(x, norm_w, w_in, conv_a_w, conv_a_b, ln_a_w, ln_a_b, w_a_out,
              conv_b_w, conv_b_b, dt_bias, a_log, d_skip, ssm_norm_w, w_b_out, w_o,
              final_norm_w):
    for i in range(DEPTH):
        h = rmsnorm(x, norm_w[i])
        proj = h @ w_in[i]
        a_val, a_gate, a_z, s_z, s_xbc, s_dt, g_a, g_b = jnp.split(proj, SPLITS, axis=-1)
        y_a = conformer_conv_branch(a_val, a_gate, a_z, conv_a_w[i], conv_a_b[i],
                                    ln_a_w[i], ln_a_b[i], w_a_out[i])
        y_b = mamba2_branch(s_z, s_xbc, s_dt, conv_b_w[i], conv_b_b[i], dt_bias[i],
                            a_log[i], d_skip[i], ssm_norm_w[i], w_b_out[i])
        merged = jax.nn.sigmoid(g_a) * y_a + jax.nn.sigmoid(g_b) * y_b
        x = x + merged @ w_o[i]
    return rmsnorm(x, final_norm_w)
```

```python
import numpy as np
from contextlib import ExitStack
import concourse.bass as bass
import concourse.mybir as mybir
from concourse.bass_utils import run_bass_kernel_spmd

F32 = mybir.dt.float32
BF16 = mybir.dt.bfloat16
AF = mybir.ActivationFunctionType
ALU = mybir.AluOpType
AX = mybir.AxisListType

D = 2048
KC = 16
T = 512
NTOK = 2048
NT = NTOK // T
A_VAL, A_GATE, A_Z, S_Z, S_X, S_B, S_C, S_DT, G_A, G_B = 0, 2048, 4096, 6144, 10240, 14336, 15360, 16384, 16448, 18496
DIN = 20544
HS = 256
NORM_EPS = 1e-6
LN_EPS = 1e-5

PC_NORMW = 0
PC_CAW = PC_NORMW + 16
PC_CAB = PC_CAW + 16 * 31
PC_LNW = PC_CAB + 16
PC_LNB = PC_LNW + 16
PC_CBW = PC_LNB + 16
PC_CBB = PC_CBW + 48 * 4
PC_SNW = PC_CBB + 48
PC_FLAG = PC_SNW + 32
NPC = PC_FLAG + 1

ENGS = ("pe", "act", "dve", "pool", "sp")


class Prog:
    LAT = 0.5
    WIN = 64

    def __init__(self, nc):
        self.nc = nc
        self.ops = []
        self.last_w = {}
        self.readers = {}
        self.last_dma = {}

    def op(self, eng, fn, reads=(), writes=(), dma=None, creads=(), cost=None):
        ex = [k for k in reads if isinstance(k, str) and k.startswith("pb")]
        if ex:
            reads = [k for k in reads if k not in ex]
            writes = list(writes) + ex
        i = len(self.ops)
        deps = {}
        for k in list(creads) + list(reads):
            w = self.last_w.get(k)
            if w is not None:
                deps[w] = True
        for k in writes:
            w = self.last_w.get(k)
            if w is not None:
                deps[w] = deps.get(w, False) or (k in ex)
            for r in self.readers.get(k, ()):
                deps.setdefault(r, False)
        odeps = set()
        if dma is not None:
            p = self.last_dma.get(dma)
            if p is not None and p not in deps:
                odeps.add(p)
            self.last_dma[dma] = i
        for k in reads:
            self.readers.setdefault(k, []).append(i)
        for k in writes:
            self.last_w[k] = i
            self.readers[k] = []
        if cost is None:
            cost = 2.0 if dma is not None else 0.3
        self.ops.append([eng, fn, deps, dma, cost, odeps])
        return i

    def schedule(self):
        ops = self.ops
        n = len(ops)
        succ = [[] for _ in range(n)]
        indeg = [0] * n
        for i, o in enumerate(ops):
            alld = set(o[2]) | o[5]
            for d in alld:
                succ[d].append(i)
            indeg[i] = len(alld)
        cp_ = [0.0] * n
        for i in range(n - 1, -1, -1):
            m_ = 0.0
            for s_ in succ[i]:
                if cp_[s_] > m_:
                    m_ = cp_[s_]
            cp_[i] = m_ + ops[i][4]
        finish = [0.0] * n
        drt = [0.0] * n
        ready = {e: [] for e in ENGS}
        for i in range(n):
            if indeg[i] == 0:
                ready[ops[i][0]].append(i)
        free = {e: 0.0 for e in ENGS}
        order = {e: [] for e in ENGS}
        done = 0
        WIN = self.WIN
        while done < n:
            best = None
            for e in ENGS:
                r = ready[e]
                if not r:
                    continue
                f = free[e]
                for i in r[:WIN]:
                    t = drt[i] if drt[i] > f else f
                    if best is None or t < best[0] - 0.05 or (t <= best[0] + 0.05 and cp_[i] > best[3]):
                        best = (t, e, i, cp_[i])
            start, e, i, _ = best
            ready[e].remove(i)
            o = ops[i]
            if o[3] is not None:
                free[e] = start + (6.0 if e == "pool" else 0.1)
                finish[i] = start + o[4]
            else:
                free[e] = start + o[4]
                finish[i] = free[e]
            order[e].append(i)
            done += 1
            for s_ in succ[i]:
                if i in ops[s_][5]:
                    t = start
                else:
                    t = finish[i] + (0.0 if (ops[s_][0] == e and o[3] is None) else self.LAT)
                if t > drt[s_]:
                    drt[s_] = t
                indeg[s_] -= 1
                if indeg[s_] == 0:
                    r = ready[ops[s_][0]]
                    lo, hi = 0, len(r)
                    while lo < hi:
                        mid = (lo + hi) // 2
                        if r[mid] < s_:
                            lo = mid + 1
                        else:
                            hi = mid
                    r.insert(lo, s_)
        self.makespan = max(finish) if n else 0.0
        return order

    def emit(self, final_waits=(), do_schedule=True):
        nc = self.nc
        ops = self.ops
        if do_schedule:
            order = self.schedule()
        else:
            order = {e: [i for i, o in enumerate(ops) if o[0] == e] for e in ENGS}
        wdeps = []
        needed = {e: set() for e in ENGS}
        for i, o in enumerate(ops):
            e = o[0]
            wl = []
            for d, is_raw in o[2].items():
                od = ops[d]
                if od[3] is None and od[0] == e:
                    if e in ("pe", "sp") or not (is_raw or o[3] is not None):
                        continue
                wl.append(d)
                if od[3] is None:
                    needed[od[0]].add(d)
            wdeps.append(wl)
        semv = {}
        for e in ENGS:
            c = 0
            for i in order[e]:
                if i in needed[e]:
                    c += 1
                    semv[i] = c
        dcount = {}
        dval = {}
        for e in ENGS:
            for i in order[e]:
                nm = ops[i][3]
                if nm is not None:
                    dcount[nm] = dcount.get(nm, 0) + 16
                    dval[i] = dcount[nm]
        with ExitStack() as st:
            esem = {e: st.enter_context(nc.semaphore("s_" + e)) for e in ENGS}
            dsem = {nm: st.enter_context(nc.semaphore("d_" + nm)) for nm in dcount}
            block = st.enter_context(nc.Block())

            def run(engname, e):
                seen = {}
                for i in order[engname]:
                    o = ops[i]
                    waits = {}
                    for d in wdeps[i]:
                        od = ops[d]
                        if od[3] is not None:
                            key, val = ("dma", od[3]), dval[d]
                        else:
                            key, val = ("eng", od[0]), semv[d]
                        if seen.get(key, -1) >= val:
                            continue
                        if waits.get(key, -1) < val:
                            waits[key] = val
                    for key, val in waits.items():
                        seen[key] = val
                        e.wait_ge(esem[key[1]] if key[0] == "eng" else dsem[key[1]], val)
                    ins = o[1](e)
                    if o[3] is not None:
                        ins.then_inc(dsem[o[3]], 16)
                    elif i in semv:
                        ins.then_inc(esem[engname], 1)
                if engname == "sp":
                    for i in final_waits:
                        e.wait_ge(dsem[ops[i][3]], dcount[ops[i][3]])

            @block.tensor
            def _(e):
                run("pe", e)

            @block.scalar
            def _(e):
                run("act", e)

            @block.vector
            def _(e):
                run("dve", e)

            @block.gpsimd
            def _(e):
                run("pool", e)

            @block.sync
            def _(e):
                run("sp", e)


def hsb_list():
    L = []
    for c0 in range(0, S_DT, HS):
        L.append(("w_in", 0, c0))
    for c0 in range(G_A, DIN, HS):
        L.append(("w_in", 0, c0))
    for c0 in range(0, D, HS):
        L.append(("w_a_out", 0, c0))
    for c0 in range(0, D, HS):
        L.append(("w_o", 0, c0))
    for c0 in range(0, D, HS):
        for kh in range(2):
            L.append(("w_b_out", kh * 2048, c0))
    return L


HSB = hsb_list()
NH = len(HSB)


def h_in(col):
    if col < S_DT:
        return col // HS, (col % HS) // 128
    c = col - G_A
    return S_DT // HS + c // HS, (c % HS) // 128


H_WA = S_DT // HS + (DIN - G_A) // HS
H_WO = H_WA + D // HS
H_WB = H_WO + D // HS


class _Stop(Exception):
    pass


def build_nc(dbg=None):
    nc = bass.Bass("TRN2", target_bir_lowering=False)
    dram_in = lambda n, s: nc.dram_tensor(n, s, F32, kind="ExternalInput").ap()
    x_main = dram_in("x_main", [NTOK, D])
    x_prev = dram_in("x_prev", [NTOK, D])
    W = {"w_in": dram_in("w_in", [D, DIN]), "w_a_out": dram_in("w_a_out", [D, D]),
         "w_b_out": dram_in("w_b_out", [2 * D, D]), "w_o": dram_in("w_o", [D, D])}
    pcol_d = dram_in("pcol", [128, NPC])
    prow_d = dram_in("prow", [128, 192])
    finw_d = dram_in("finw", [128, D])
    out_d = nc.dram_tensor("out", [NTOK, D], F32, kind="ExternalOutput").ap()
    wsc = nc.dram_tensor("wsc", [NH, 128, KC, HS], BF16).ap()
    dbg_d = nc.dram_tensor("dbg", [128, 16384], F32, kind="ExternalOutput").ap() if dbg else None

    P = Prog(nc)
    with ExitStack() as st:
        sbt = lambda n, s, d: st.enter_context(nc.sbuf_tensor("sb_" + n, s, d))
        pst = lambda n, s, d: st.enter_context(nc.psum_tensor(n, s, d))
        pb = [pst("pb%d" % i, [128, 512], F32) for i in range(7)]
        pb7 = pst("pb7", [128, 1024], BF16)
        hT = sbt("hT", [128, KC, T], BF16)
        NSLOT = 3
        wslot = [sbt("wslot%d" % i, [128, KC, HS], BF16) for i in range(NSLOT)]
        R1 = sbt("R1", [128, 16 * T], F32)
        va = sbt("va", [128, KC, T], BF16)
        ma = sbt("ma", [128, KC, T], BF16)
        BT2 = [sbt("BT%d" % i, [128, 2, T], BF16) for i in range(2)]
        CT2 = [sbt("CT%d" % i, [128, 2, T], BF16) for i in range(2)]
        xTg2 = [sbt("xTg%d" % i, [128, 4, T], BF16) for i in range(2)]
        zsgbuf = sbt("zsgbuf", [128, 2 * 4 * T], BF16)
        zsg2 = [zsgbuf[:, i * 4 * T:(i + 1) * 4 * T].rearrange("p (b t) -> p b t", b=4) for i in range(2)]
        te2 = [sbt("te%d" % i, [128, 256], F32) for i in range(2)]
        dec2 = [sbt("dec%d" % i, [128, 256], F32) for i in range(2)]
        stT = sbt("stT", [128, 8, 512], F32)
        stb = sbt("stb", [128, 512], BF16)
        R3 = sbt("R3", [128, 2048], F32)
        R2 = sbt("R2", [128, 2048], F32)
        upad = sbt("upad", [128, 30 + T], BF16)
        ucarry = sbt("ucarry", [128, 16, 30], BF16)
        sgt = sbt("sgt", [128, T], F32)
        sqb2 = [sbt("sqb%d" % i, [128, T], BF16) for i in range(2)]
        ucb2 = [sbt("ucb%d" % i, [128, T], BF16) for i in range(2)]
        rstd = sbt("rstd", [128, T], F32)
        saz = sbt("saz", [128, T], F32)
        xrawb = [sbt("xrawb%d" % i, [128, 4 + T], BF16) for i in range(2)]
        diag4 = [sbt("diag4_%d" % i, [128, 4, 128], BF16) for i in range(2)]
        cacc = sbt("cacc", [128, T], F32)
        carryb = sbt("carryb", [128, 48, 3], BF16)
        Zhi = sbt("Zhi", [128, 8, 128], BF16)
        Zlo = sbt("Zlo", [128, 8, 128], BF16)
        cbTm4 = sbt("cbTm4", [128, 4, 128], BF16)
        xp2 = [sbt("xp%d" % i, [128, 512], BF16) for i in range(2)]
        xpp2 = [sbt("xpp%d" % i, [128, 512], BF16) for i in range(2)]
        btm2 = [sbt("btm%d" % i, [128, 128], BF16) for i in range(2)]
        dtt = sbt("dtt", [128, 256], F32)
        da = sbt("da", [128, 256], F32)
        cum = sbt("cum", [128, 256], F32)
        ecum = sbt("ecum", [128, 256], F32)
        dahi = sbt("dahi", [128, 256], BF16)
        dalo = sbt("dalo", [128, 256], BF16)
        ssc = sbt("ssc", [128, 2], F32)
        ssc2 = sbt("ssc2", [128, 2], F32)
        ssc3 = sbt("ssc3", [128, 2], F32)
        ones_b = sbt("ones_b", [128, 128], BF16)
        dummy = sbt("dmy_t", [128, 2], F32)
        pcol = sbt("pcol", [128, NPC], F32)
        prow = sbt("prow", [128, 192], F32)
        Arow = sbt("Arow", [128, 64], F32)
        wdt = sbt("wdt", [128, KC, 64], BF16)
        ident_f = sbt("ident_f", [128, 128], F32)
        ones_f = sbt("ones_f", [128, 128], F32)
        U_f = sbt("U_f", [128, 128], F32)
        L_f = sbt("L_f", [128, 128], F32)
        ident_b = sbt("ident_b", [128, 128], BF16)
        U_b = sbt("U_b", [128, 128], BF16)
        L_b = sbt("L_b", [128, 128], BF16)

        msq = sgt
        dtx, dta, dtl, dahf = sgt[:, 0:256], sgt[:, 256:512], saz[:, 0:256], saz[:, 256:512]
        R2b = R2[:].bitcast(BF16)
        M1q = [R2b[:, 0:1024].rearrange("p (r l) -> p r l", r=8), R2b[:, 1024:2048].rearrange("p (r l) -> p r l", r=8)]
        ysb = R2[:, 1024:1536]
        xDq = [R2b[:, 3072:3584], R2b[:, 3584:4096]]
        mu = cacc
        t1 = cacc
        uconv = R1[:].rearrange("p (j t) -> p j t", j=16)
        ynT = R1[:].bitcast(BF16).rearrange("p (j t) -> p j t", j=32)
        xres = R1[:].rearrange("p (b d) -> p b d", b=4)
        xin = R3
        yg = R3[:].rearrange("p (b t) -> p b t", b=4)
        xinB = zsgbuf[:].bitcast(F32)
        ma_f = ma[:].rearrange("p a b -> p (a b)").bitcast(F32)
        finw_v, sq_v = ma_f[:, 0:2048], ma_f[:, 2048:4096]
        sqtmp = R2
        diag = R2[:].bitcast(BF16)[:, 0:31 * 128].rearrange("p (k c) -> p k c", k=31)

        C = ["pcol", "prow", "ident_f", "ones_f", "U_f", "L_f", "ident_b", "U_b", "L_b", "Arow", "wdt", "ones_b"]

        def handover(keys):
            P.op("dve", lambda e: e.tensor_copy(out=dummy[:, 0:1], in_=dummy[:, 1:2]), [], keys)

        def fsz(ap):
            n_ = 1
            for d_ in ap.shape[1:]:
                n_ *= d_
            return n_

        def mm(out, lhsT, rhs, start, stop, r, w, cr=()):
            c_ = max(0.07, fsz(rhs) / 2200.0 * (4 if rhs.dtype == F32 else 1))
            P.op("pe", lambda e: e.matmul(out, lhsT=lhsT, rhs=rhs, start=start, stop=stop), r, w, creads=cr, cost=c_)

        def tr(out, in_, ident, r, w):
            P.op("pe", lambda e: e.transpose(out=out, in_=in_, identity=ident), r, w, creads=C,
                 cost=0.25 if in_.dtype == F32 else 0.08)

        def act(out, in_, func, r, w, scale=None, bias=None, eng="act"):
            kw = {}
            if scale is not None:
                kw["scale"] = scale
            if bias is not None:
                kw["bias"] = bias
            P.op(eng, lambda e: e.activation(out=out, in_=in_, func=func, **kw), r, w, creads=C, cost=0.45 + fsz(out) / 1150.0)

        def vcost(eng, out):
            return (0.15 + fsz(out) / 900.0) if eng == "dve" else (0.3 + fsz(out) / 450.0)

        def tt(eng, out, in0, in1, op, r, w):
            P.op(eng, lambda e: e.tensor_tensor(out=out, in0=in0, in1=in1, op=op), r, w, creads=C, cost=vcost(eng, out))

        def ts(eng, out, in0, s1, s2, op0, op1, r, w):
            if op1 is None:
                P.op(eng, lambda e: e.tensor_scalar(out=out, in0=in0, scalar1=s1, scalar2=None, op0=op0), r, w, creads=C,
                     cost=vcost(eng, out))
            else:
                P.op(eng, lambda e: e.tensor_scalar(out=out, in0=in0, scalar1=s1, scalar2=s2, op0=op0, op1=op1), r, w, creads=C,
                     cost=vcost(eng, out))

        def stt(eng, out, in0, scalar, in1, op0, op1, r, w):
            P.op(eng, lambda e: e.scalar_tensor_tensor(out=out, in0=in0, scalar=scalar, in1=in1, op0=op0, op1=op1), r, w,
                 creads=C, cost=vcost(eng, out))

        def cp(eng, out, in_, r, w):
            P.op(eng, lambda e: e.tensor_copy(out=out, in_=in_), r, w, creads=C, cost=vcost(eng, out))

        def recip(ap_, r, w):
            P.op("dve", lambda e: e.reciprocal(out=ap_, in_=ap_), r, w, cost=0.15 + fsz(ap_) * 6.5 / 960.0)

        def dma(eng, out, in_, r, w, sem):
            nbytes = fsz(out) * out.shape[0] * (2 if out.dtype == BF16 else 4)
            return P.op(eng, lambda e: e.dma_start(out=out, in_=in_), r, w, dma=sem, cost=2.0 + nbytes / 200e3)

        def pc(i):
            return pcol[:, i:i + 1]

        dma("sp", pcol[:], pcol_d, [], ["pcol"], "c0")
        dma("sp", prow[:], prow_d, [], ["prow"], "c1")
        P.op("dve", lambda e: e.memset(ident_f[:], 0.0), [], ["ident_f"])
        P.op("dve", lambda e: e.memset(ones_f[:], 1.0), [], ["ones_f"])
        P.op("pool", lambda e: e.affine_select(out=ident_f[:], in_=ident_f[:], compare_op=ALU.not_equal, fill=1.0,
                                               base=0, pattern=[[-1, 128]], channel_multiplier=1), ["ident_f"], ["ident_f"])
        P.op("pool", lambda e: e.affine_select(out=U_f[:], in_=ones_f[:], compare_op=ALU.is_ge, fill=0.0,
                                               base=0, pattern=[[1, 128]], channel_multiplier=-1), ["ones_f"], ["U_f"])
        P.op("pool", lambda e: e.affine_select(out=L_f[:], in_=ones_f[:], compare_op=ALU.is_ge, fill=0.0,
                                               base=-1, pattern=[[-1, 128]], channel_multiplier=1), ["ones_f"], ["L_f"])
        cp("dve", ident_b[:], ident_f[:], ["ident_f"], ["ident_b"])
        cp("dve", U_b[:], U_f[:], ["U_f"], ["U_b"])
        cp("dve", L_b[:], L_f[:], ["L_f"], ["L_b"])
        cp("dve", ones_b[:], ones_f[:], ["ones_f"], ["ones_b"])
        P.op("dve", lambda e: e.memset(stT[:], 0.0), [], ["stT"])
        P.op("dve", lambda e: e.memset(ucarry[:], 0.0), [], [("ucarry", j) for j in range(16)])
        P.op("dve", lambda e: e.memset(carryb[:], 0.0), [], [("carryb", j) for j in range(48)])
        act(Arow[:], prow[:, 64:128], AF.Exp, ["prow"], ["Arow"])
        ts("dve", Arow[:], Arow[:], -1.0, None, ALU.mult, None, ["Arow"], ["Arow"])
        P.op("dve", lambda e: e.memset(dummy[:], 0.0), [], [])
        dma("pool", wdt[:], W["w_in"][:, S_DT:S_DT + 64].rearrange("(kc p) c -> p kc c", p=128), [], ["wdt"], "cwdt")

        def cast_h(h):
            name, k0, c0 = HSB[h]
            src = W[name][k0:k0 + 2048, c0:c0 + HS].rearrange("(kc p) c -> p kc c", p=128)
            dma("pool", wsc[h], src, [], [("wsc", h)], "cast%d" % (h % 56))

        order = []
        xh = [h_in(S_X + i * HS)[0] for i in range(16)]
        bh = [h_in(S_B + i * HS)[0] for i in range(4)]
        for g in range(8):
            if g % 2 == 0:
                order.append(bh[g // 2])
            order += xh[2 * g:2 * g + 2]
        order += [h_in(A_GATE + i * HS)[0] for i in range(8)]
        order += [h_in(A_VAL + i * HS)[0] for i in range(8)]
        for h in range(NH):
            if h not in order:
                order.append(h)
        if dbg and "ncast" in dbg:
            order = order[:dbg["ncast"]]
        for h in order:
            cast_h(h)

        wctr = [0]
        xbc_ctr = [0]

        def getw(h):
            s = wctr[0] % NSLOT
            wctr[0] += 1
            dma("sp", wslot[s][:], wsc[h], [("wsc", h)], [("ws", s)], "wl%d" % s)
            return wslot[s], ("ws", s)

        bank_rr = [0]

        def inproj(wt, wk, cb, bank, t_lo=0):
            for kc in range(KC):
                mm(pb[bank][:, t_lo:T], wt[:, kc, cb * 128:(cb + 1) * 128], hT[:, kc, t_lo:T], kc == 0, kc == KC - 1,
                   [wk, "hT"], ["pb%d" % bank])

        banks = [[0, 1]]
        tctr = [0]

        def setbanks(l):
            banks[0] = [0, 1]

        def nextbank():
            b = banks[0][bank_rr[0] % len(banks[0])]
            bank_rr[0] += 1
            return b

        def stage_end(name, mode):
            if dbg and dbg.get("stop") == name and dbg.get("mode") == mode:
                raise _Stop()

        def tile(x_src, t0, mode, out_ap=None):
            main = mode == "main"
            tctr[0] += 1
            te, dec = te2[tctr[0] % 2], dec2[tctr[0] % 2]
            kte, kdec = ("te", tctr[0] % 2), ("dec", tctr[0] % 2)
            setbanks([0, 1])
            handover([("zsg", 0), ("zsg", 1), "xinB"])
            for tb in range(4):
                xin_, kxin = (R3[:], "R3") if tb % 2 == 0 else (xinB, "xinB")
                sc_, ksc = (ssc, "ssc") if tb % 2 == 0 else (ssc3, "ssc3")
                dma("sp", xin_, x_src[t0 + tb * 128:t0 + (tb + 1) * 128, :], [], [kxin], "xin%d" % (tb % 2))
                act(sqtmp[:], xin_, AF.Square, [kxin], ["R2"])
                P.op("dve", lambda e, sc_=sc_: e.reduce_sum(out=sc_[:, 0:1], in_=sqtmp[:], axis=AX.X), ["R2"], [ksc], cost=2.4)
                ts("dve", sc_[:, 1:2], sc_[:, 0:1], 1.0 / D, NORM_EPS, ALU.mult, ALU.add, [ksc], [ksc])
                act(sc_[:, 1:2], sc_[:, 1:2], AF.Sqrt, [ksc], [ksc])
                recip(sc_[:, 1:2], [ksc], [ksc])
                ts("dve", xin_, xin_, sc_[:, 1:2], None, ALU.mult, None, [kxin, ksc], [kxin])
                for kc in range(KC):
                    q = kc // 4
                    tr(pb[2 + q][:, (kc % 4) * 128:(kc % 4 + 1) * 128], xin_[:, kc * 128:(kc + 1) * 128], ident_f[:],
                       [kxin], ["pb%d" % (2 + q)])
                for kc in range(KC):
                    q = kc // 4
                    act(hT[:, kc, tb * 128:(tb + 1) * 128], pb[2 + q][:, (kc % 4) * 128:(kc % 4 + 1) * 128], AF.Copy,
                        ["pb%d" % (2 + q)], ["hT"], scale=pc(PC_NORMW + kc))

            handover([("zsg", 0), ("zsg", 1), "xinB"])
            stage_end("H", mode)
            for c in range(4):
                for kc in range(KC):
                    mm(pb[2][:, c * 64:(c + 1) * 64], hT[:, kc, c * 128:(c + 1) * 128], wdt[:, kc, :], kc == 0, kc == KC - 1,
                       ["hT"], ["pb2"], cr=["wdt"])
            v3 = lambda a: a.rearrange("p (c r) -> p c r", c=4)
            tt("dve", v3(dtx[:]), v3(pb[2][:, 0:256]), prow[:, 0:64].unsqueeze(1).to_broadcast([128, 4, 64]), ALU.add,
               ["pb2"], ["sgt"])
            ts("dve", dtl[:], dtx[:], 0.0, None, ALU.max, None, ["sgt"], ["saz"])
            stt("dve", dta[:], dtl[:], -2.0, dtx[:], ALU.mult, ALU.add, ["sgt", "saz"], ["sgt"])
            act(dta[:], dta[:], AF.Exp, ["sgt"], ["sgt"])
            ts("dve", dta[:], dta[:], 1.0, None, ALU.add, None, ["sgt"], ["sgt"])
            act(dta[:], dta[:], AF.Ln, ["sgt"], ["sgt"])
            tt("dve", dtt[:], dtl[:], dta[:], ALU.add, ["sgt", "saz"], ["dtt"])
            tt("dve", v3(da[:]), v3(dtt[:]), Arow[:].unsqueeze(1).to_broadcast([128, 4, 64]), ALU.mult, ["dtt"], ["da"])
            mm(pb[3][:, 0:256], U_f[:], da[:], True, True, ["da"], ["pb3"], cr=C)
            mm(pb[4][:, 0:256], ones_f[:], da[:], True, True, ["da"], ["pb4"], cr=C)
            cp("dve", cum[:], pb[3][:, 0:256], ["pb3"], ["cum"])
            act(ecum[:], cum[:], AF.Exp, ["cum"], ["ecum"])
            tt("dve", te[:], pb[4][:, 0:256], cum[:], ALU.subtract, ["pb4", "cum"], [kte])
            act(te[:], te[:], AF.Exp, [kte], [kte])
            tt("dve", te[:], te[:], dtt[:], ALU.mult, [kte, "dtt"], [kte])
            act(dec[:], pb[4][:, 0:256], AF.Exp, ["pb4"], [kdec])
            cp("dve", dahi[:], da[:], ["da"], ["dahi"])
            cp("dve", dahf[:], dahi[:], ["dahi"], ["saz"])
            tt("dve", dalo[:], da[:], dahf[:], ALU.subtract, ["da", "saz"], ["dalo"])

            stage_end("DT", mode)
            setbanks([0, 1, 3, 4] if main else [0, 1])
            def a1_inproj(j):
                if j % 2 == 0:
                    a1_inproj.wg = getw(h_in(A_GATE + j * 128)[0])
                    a1_inproj.wv = getw(h_in(A_VAL + j * 128)[0])
                cb = j % 2
                lo = 0 if main else T - 128
                b0 = nextbank()
                inproj(a1_inproj.wg[0], a1_inproj.wg[1], cb, b0, lo)
                act(sgt[:, lo:T], pb[b0][:, lo:T], AF.Sigmoid, ["pb%d" % b0], ["sgt"])
                b1 = nextbank()
                inproj(a1_inproj.wv[0], a1_inproj.wv[1], cb, b1, lo)
                if main:
                    cp("dve", upad[:, 0:30], ucarry[:, j, :], [("ucarry", j)], ["upad"])
                tt("dve", upad[:, 30 + lo:30 + T], pb[b1][:, lo:T], sgt[:, lo:T], ALU.mult, ["pb%d" % b1, "sgt"], ["upad"])
                cp("dve", ucarry[:, j, :], upad[:, T:T + 30], ["upad"], [("ucarry", j)])

            def a1_conv(j):
                P.op("dve", lambda e: e.tensor_tensor(
                    out=diag, in0=ident_b[:].unsqueeze(1).to_broadcast([128, 31, 128]),
                    in1=pcol[:, PC_CAW + j * 31:PC_CAW + (j + 1) * 31].unsqueeze(2).to_broadcast([128, 31, 128]),
                    op=ALU.mult), [], ["R2"], creads=C, cost=4.4)
                for k in range(31):
                    mm(pb[2][:], diag[:, k, :], upad[:, k:k + T], k == 0, k == 30, ["R2", "upad"], ["pb2"])
                act(uconv[:, j, :], pb[2][:], AF.Identity, ["pb2"], [("R1", j)], bias=pc(PC_CAB + j))
                act(ucb2[j % 2][:], pb[2][:], AF.Identity, ["pb2"], [("ucb", j % 2)], bias=pc(PC_CAB + j))
                act(sqb2[j % 2][:], uconv[:, j, :], AF.Square, [("R1", j)], [("sqb", j % 2)])
                mm(pb[5][:], ones_b[:], ucb2[j % 2][:], j == 0, j == 15, [("ucb", j % 2)], ["pb5"], cr=C)
                mm(pb[6][:], ones_b[:], sqb2[j % 2][:], j == 0, j == 15, [("sqb", j % 2)], ["pb6"], cr=C)

            if main or mode == "pre_last":
                for j in range(16):
                    a1_inproj(j)
                    if main:
                        a1_conv(j)
            stage_end("A1", mode)
            if main:
                ts("dve", mu[:], pb[5][:], 1.0 / D, None, ALU.mult, None, ["pb5"], ["cacc"])
                tt("dve", msq[:], mu[:], mu[:], ALU.mult, ["cacc"], ["sgt"])
                stt("dve", rstd[:], pb[6][:], 1.0 / D, msq[:], ALU.mult, ALU.subtract, ["pb6", "sgt"], ["rstd"])
                ts("dve", rstd[:], rstd[:], LN_EPS, None, ALU.add, None, ["rstd"], ["rstd"])
                act(rstd[:], rstd[:], AF.Sqrt, ["rstd"], ["rstd"])
                recip(rstd[:], ["rstd"], ["rstd"])
                for j in range(16):
                    if j % 2 == 0:
                        wz = getw(h_in(A_Z + j * 128)[0])
                    b0 = nextbank()
                    inproj(wz[0], wz[1], j % 2, b0)
                    act(saz[:], pb[b0][:], AF.Silu, ["pb%d" % b0], ["saz"])
                    tt("dve", uconv[:, j, :], uconv[:, j, :], mu[:], ALU.subtract, [("R1", j), "cacc"], [("R1", j)])
                    tt("dve", uconv[:, j, :], uconv[:, j, :], rstd[:], ALU.mult, [("R1", j), "rstd"], [("R1", j)])
                    act(uconv[:, j, :], uconv[:, j, :], AF.Silu, [("R1", j)], [("R1", j)], scale=pc(PC_LNW + j), bias=pc(PC_LNB + j))
                    tt("dve", va[:, j, :], uconv[:, j, :], saz[:], ALU.mult, [("R1", j), "saz"], [("va", j)])
                stage_end("A2", mode)
                for jo in range(16):
                    if jo % 2 == 0:
                        wa = getw(H_WA + jo // 2)
                        wg = getw(h_in(G_A + jo * 128)[0])
                    cb = jo % 2
                    b0 = nextbank()
                    inproj(wg[0], wg[1], cb, b0)
                    act(sgt[:], pb[b0][:], AF.Sigmoid, ["pb%d" % b0], ["sgt"])
                    b1 = nextbank()
                    for kc in range(KC):
                        mm(pb[b1][:], wa[0][:, kc, cb * 128:(cb + 1) * 128], va[:, kc, :], kc == 0, kc == KC - 1,
                           [wa[1]] + [("va", kc)], ["pb%d" % b1])
                    tt("dve", ma[:, jo, :], pb[b1][:], sgt[:], ALU.mult, ["pb%d" % b1, "sgt"], [("ma", jo)])

            stage_end("A3", mode)
            setbanks([0, 1] if main else [0, 1, 2, 3, 5, 6])
            def xbc_block(wt, wk, cb, blk, dst, dkey):
                q = xbc_ctr[0] % 2
                xbc_ctr[0] += 1
                b0 = nextbank()
                inproj(wt, wk, cb, b0)
                cp("dve", xrawb[q][:, 0:3], carryb[:, blk, :], [("carryb", blk)], [("xraw", q)])
                act(xrawb[q][:, 3:3 + T], pb[b0][:], AF.Copy, ["pb%d" % b0], [("xraw", q)])
                cp("dve", carryb[:, blk, :], xrawb[q][:, T:T + 3], [("xraw", q)], [("carryb", blk)])
                P.op("dve", lambda e: e.tensor_tensor(
                    out=diag4[q][:], in0=ident_b[:].unsqueeze(1).to_broadcast([128, 4, 128]),
                    in1=pcol[:, PC_CBW + blk * 4:PC_CBW + blk * 4 + 4].unsqueeze(2).to_broadcast([128, 4, 128]),
                    op=ALU.mult), [], [("diag4", q)], creads=C, cost=0.7)
                b1 = b0
                for k in range(4):
                    mm(pb[b1][:], diag4[q][:, k, :], xrawb[q][:, k:k + T], k == 0, k == 3, [("diag4", q), ("xraw", q)], ["pb%d" % b1])
                act(dst, pb[b1][:], AF.Silu, ["pb%d" % b1], [dkey], bias=pc(PC_CBB + blk))

            pre_last = mode == "pre_last"

            def prep_list(g):
                L = []
                par, pp = g % 2, (g // 2) % 2
                hold = {}
                if par == 0:
                    def fB(cb):
                        if cb == 0:
                            hold["B"] = getw(h_in(S_B + g * 128)[0])
                        xbc_block(hold["B"][0], hold["B"][1], cb, 32 + g + cb, BT2[pp][:, cb, :], ("BT", pp))
                    L += [lambda cb=cb: fB(cb) for cb in range(2)]
                    if main or pre_last:
                        def fC(cb):
                            if cb == 0:
                                hold["C"] = getw(h_in(S_C + g * 128)[0])
                            if main:
                                xbc_block(hold["C"][0], hold["C"][1], cb, 40 + g + cb, CT2[pp][:, cb, :], ("CT", pp))
                            else:
                                b0 = nextbank()
                                inproj(hold["C"][0], hold["C"][1], cb, b0, T - 128)
                                act(carryb[:, 40 + g + cb, :], pb[b0][:, T - 3:T], AF.Copy, ["pb%d" % b0], [("carryb", 40 + g + cb)])
                        L += [lambda cb=cb: fC(cb) for cb in range(2)]

                def fx(b):
                    if b % 2 == 0:
                        hold["x"] = getw(h_in(S_X + (4 * g + b) * 128)[0])
                    xbc_block(hold["x"][0], hold["x"][1], b % 2, 4 * g + b, xTg2[par][:, b, :], ("xTg", par))
                L += [lambda b=b: fx(b) for b in range(4)]
                if main:
                    def fz(b):
                        if b % 2 == 0:
                            hold["z"] = getw(h_in(S_Z + (4 * g + b) * 128)[0])
                        b0 = nextbank()
                        inproj(hold["z"][0], hold["z"][1], b % 2, b0)
                        act(zsg2[par][:, b, :], pb[b0][:], AF.Silu, ["pb%d" % b0], [("zsg", par)])
                    L += [lambda b=b: fz(b) for b in range(4)]
                return L

            for f in prep_list(0):
                f()
            if main:
                handover(["R2", "M1a", "M1b", "ysb", "xDa", "xDb"])
            h3 = lambda a: a.rearrange("p (r q) -> p r q", r=8)
            bc_p = lambda a: a.unsqueeze(2).to_broadcast([128, 8, 64])
            M1k, xDk = ["M1a", "M1b"], ["xDa", "xDb"]

            def grp(g):
                par, pp = g % 2, (g // 2) % 2
                return dict(gl=g % 2, xTg=xTg2[par], zsg=zsg2[par], BT=BT2[pp], CT=CT2[pp],
                            kx=("xTg", par), kz=("zsg", par), kB=("BT", pp), kC=("CT", pp))

            def cbT4(g):
                G = grp(g)
                for c in range(4):
                    cs = slice(c * 128, (c + 1) * 128)
                    mm(pb[4][:, cs], G["BT"][:, G["gl"], cs], G["CT"][:, G["gl"], cs], True, True, [G["kB"], G["kC"]], ["pb4"])
                tt("dve", cbTm4[:], pb[4][:].rearrange("p (c l) -> p c l", c=4), U_f[:].unsqueeze(1).to_broadcast([128, 4, 128]),
                   ALU.mult, ["pb4"], ["cbTm4"])

            def S1(g, c, q):
                G = grp(g)
                cs = slice(c * 128, (c + 1) * 128)
                rs = slice(c * 64 + g * 8, c * 64 + g * 8 + 8)
                for b in range(4):
                    tr(pb7[:, b * 128:(b + 1) * 128], G["xTg"][:, b, cs], ident_b[:], [G["kx"]], ["pb7"])
                tr(pb7[:, 512:640], G["BT"][:, G["gl"], cs], ident_b[:], [G["kB"]], ["pb7"])
                cp("dve", btm2[q][:], pb7[:, 512:640], ["pb7"], [("btm", q)])
                tt("dve", h3(xpp2[q][:]), h3(pb7[:, 0:512]), bc_p(te[:, rs]), ALU.mult, ["pb7", kte], [("xpp", q)])
                if not main:
                    return
                tt("dve", h3(xp2[q][:]), h3(pb7[:, 0:512]), bc_p(dtt[:, rs]), ALU.mult, ["pb7", "dtt"], [("xp", q)])
                tt("dve", h3(xDq[q]), h3(pb7[:, 0:512]), bc_p(prow[:, 128 + g * 8:128 + g * 8 + 8]), ALU.mult,
                   ["pb7"], [xDk[q]])
                ub = U_b[:].unsqueeze(1).to_broadcast([128, 8, 128])
                tt("pool", Zhi[:], ub, dahi[:, rs].unsqueeze(2).to_broadcast([128, 8, 128]), ALU.mult, ["dahi"], ["Zhi"])
                for hh in range(2):
                    zs_ = slice(hh * 4, hh * 4 + 4)
                    mm(pb[5 + hh][:], L_b[:], Zhi[:, zs_, :].rearrange("p r l -> p (r l)"), True, True, ["Zhi"], ["pb%d" % (5 + hh)], cr=C)
                    act(M1q[q][:, zs_, :].rearrange("p r l -> p (r l)"), pb[5 + hh][:], AF.Exp, ["pb%d" % (5 + hh)], [M1k[q]])
                tt("dve", M1q[q], M1q[q], cbTm4[:, c, :].unsqueeze(1).to_broadcast([128, 8, 128]), ALU.mult,
                   [M1k[q], "cbTm4"], [M1k[q]])

            seq = [(g, c) for g in range(8) for c in range(4)]
            nxt = prep_list(1)

            def fill(n=1):
                for _ in range(n):
                    if nxt:
                        nxt.pop(0)()

            if main:
                cbT4(0)
            S1(0, 0, 0)
            for n, (g, c) in enumerate(seq):
                q = n % 2
                G = grp(g)
                cs = slice(c * 128, (c + 1) * 128)
                rs = slice(c * 64 + g * 8, c * 64 + g * 8 + 8)
                if main:
                    mm(pb[2][:], ident_b[:], xDq[q], True, False, [xDk[q]], ["pb2"], cr=C)
                    for r in range(8):
                        mm(pb[2][:, r * 64:(r + 1) * 64], M1q[q][:, r, :], xp2[q][:, r * 64:(r + 1) * 64], False, r == 7,
                           [M1k[q], ("xp", q)], ["pb2"])
                    act(stb[:], stT[:, g, :], AF.Copy, [("stT", g)], ["stb"])
                    mm(pb[3][:], G["CT"][:, G["gl"], cs], stb[:], True, True, [G["kC"], "stb"], ["pb3"])
                mm(pb[4][:], btm2[q][:], xpp2[q][:], True, True, [("btm", q), ("xpp", q)], ["pb4"])
                if main:
                    tt("dve", h3(ysb), h3(pb[3][:]), bc_p(ecum[:, rs]), ALU.mult, ["pb3", "ecum"], ["ysb"])
                    tt("dve", ysb, ysb, pb[2][:], ALU.add, ["ysb", "pb2"], ["ysb"])
                tt("dve", h3(stT[:, g, :]), h3(stT[:, g, :]), bc_p(dec[:, rs]), ALU.mult, [("stT", g), kdec], [("stT", g)])
                tt("dve", stT[:, g, :], stT[:, g, :], pb[4][:], ALU.add, [("stT", g), "pb4"], [("stT", g)])
                fill(1 if main else (2 if c % 2 == 0 else 1))
                if main:
                    for b in range(4):
                        tr(pb[3][:, b * 128:(b + 1) * 128], ysb[:, b * 128:(b + 1) * 128], ident_f[:], ["ysb"], ["pb3"])
                    tt("dve", yg[:, :, cs], pb[3][:].rearrange("p (b l) -> p b l", b=4), G["zsg"][:, :, cs], ALU.mult,
                       ["pb3", G["kz"]], ["R3"])
                if c == 3:
                    while nxt:
                        fill(1)
                    if main:
                        for b in range(4):
                            act(sqb2[b % 2][:], yg[:, b, :], AF.Square, ["R3"], [("sqb", b % 2)])
                            mm(pb[4][:], ones_b[:], sqb2[b % 2][:], b == 0, b == 3, [("sqb", b % 2)], ["pb4"], cr=C)
                        ts("dve", rstd[:], pb[4][:], 1.0 / 512, NORM_EPS, ALU.mult, ALU.add, ["pb4"], ["rstd"])
                        act(rstd[:], rstd[:], AF.Sqrt, ["rstd"], ["rstd"])
                        recip(rstd[:], ["rstd"], ["rstd"])
                        for b in range(4):
                            if g == 0 and b == 0:
                                handover([("R1", j) for j in range(16)] + [("yn", k) for k in range(32)])
                            stt("dve", ynT[:, 4 * g + b, :], yg[:, b, :], pc(PC_SNW + 4 * g + b), rstd[:], ALU.mult, ALU.mult,
                                ["R3", "rstd"], [("yn", 4 * g + b)])
                    if g < 7:
                        nxt.extend(prep_list(g + 2) if g + 2 < 8 else [])
                        if main:
                            cbT4(g + 1)
                if n + 1 < len(seq):
                    S1(seq[n + 1][0], seq[n + 1][1], (n + 1) % 2)
                fill(1 if main else 0)
            while nxt:
                fill(1)
            if main:
                handover(["R2", "M1a", "M1b", "ysb", "xDa", "xDb"])
            stage_end("B", mode)
            if not main:
                return
            setbanks([0, 1, 2, 3, 4, 5, 6])
            for jo in range(16):
                if jo % 2 == 0:
                    wb0 = getw(H_WB + (jo // 2) * 2)
                    wb1 = getw(H_WB + (jo // 2) * 2 + 1)
                    wg = getw(h_in(G_B + jo * 128)[0])
                cb = jo % 2
                b0 = nextbank()
                inproj(wg[0], wg[1], cb, b0)
                act(sgt[:], pb[b0][:], AF.Sigmoid, ["pb%d" % b0], ["sgt"])
                b1 = nextbank()
                for kc in range(32):
                    wb_ = wb0 if kc < 16 else wb1
                    mm(pb[b1][:], wb_[0][:, kc % 16, cb * 128:(cb + 1) * 128], ynT[:, kc, :], kc == 0, kc == 31,
                       [wb_[1], ("yn", kc)], ["pb%d" % b1])
                tt("dve", saz[:], pb[b1][:], sgt[:], ALU.mult, ["pb%d" % b1, "sgt"], ["saz"])
                tt("dve", va[:, jo, :], saz[:], ma[:, jo, :], ALU.add, ["saz", ("ma", jo)], [("va", jo)])
            stage_end("M", mode)
            handover([("yn", k) for k in range(32)] + [("xr", tb) for tb in range(4)])
            for tb in range(4):
                dma("sp", xres[:, tb, :], x_src[t0 + tb * 128:t0 + (tb + 1) * 128, :], [], [("xr", tb)], "xr%d" % tb)
            handover([("ma", j) for j in range(16)] + ["maF", "maS"])
            dma("sp", finw_v, finw_d, [], ["maF"], "finw")
            for hq in range(8):
                wo = getw(H_WO + hq)
                for tb in range(4):
                    b0 = nextbank()
                    for kc in range(KC):
                        mm(pb[b0][:, 0:HS], va[:, kc, tb * 128:(tb + 1) * 128], wo[0][:, kc, :], kc == 0, kc == KC - 1,
                           [wo[1], ("va", kc)], ["pb%d" % b0])
                    tt("dve", xres[:, tb, hq * HS:(hq + 1) * HS], xres[:, tb, hq * HS:(hq + 1) * HS], pb[b0][:, 0:HS], ALU.add,
                       [("xr", tb), "pb%d" % b0], [("xr", tb)])
            toks = []
            for tb in range(4):
                act(sq_v, xres[:, tb, :], AF.Square, [("xr", tb)], ["maS"])
                P.op("dve", lambda e: e.reduce_sum(out=ssc2[:, 0:1], in_=sq_v, axis=AX.X), ["maS"], ["ssc2"], cost=2.4)
                ts("dve", ssc2[:, 1:2], ssc2[:, 0:1], 1.0 / D, NORM_EPS, ALU.mult, ALU.add, ["ssc2"], ["ssc2"])
                act(ssc2[:, 1:2], ssc2[:, 1:2], AF.Sqrt, ["ssc2"], ["ssc2"])
                recip(ssc2[:, 1:2], ["ssc2"], ["ssc2"])
                stt("dve", xres[:, tb, :], xres[:, tb, :], ssc2[:, 1:2], finw_v, ALU.mult, ALU.mult,
                    [("xr", tb), "ssc2", "maF"], [("xr", tb)])
                toks.append(dma("sp", out_ap[t0 + tb * 128:t0 + (tb + 1) * 128, :], xres[:, tb, :], [("xr", tb)], [], "out"))
            handover([("xr", tb) for tb in range(4)] + [("R1", j) for j in range(16)])
            handover([("ma", j) for j in range(16)] + ["maF", "maS"])
            return toks

        def program():
            if not (dbg and dbg.get("nopre")):
                for ti in range(NT):
                    tile(x_prev, ti * T, "pre_last" if ti == NT - 1 else "pre")
                for g in range(8):
                    ts("dve", stT[:, g, :], stT[:, g, :], pc(PC_FLAG), None, ALU.mult, None, [("stT", g)], [("stT", g)])
                ts("dve", ucarry[:].rearrange("p j t -> p (j t)"), ucarry[:].rearrange("p j t -> p (j t)"), pc(PC_FLAG), None,
                   ALU.mult, None, [("ucarry", j) for j in range(16)], [("ucarry", j) for j in range(16)])
                ts("dve", carryb[:].rearrange("p j t -> p (j t)"), carryb[:].rearrange("p j t -> p (j t)"), pc(PC_FLAG), None,
                   ALU.mult, None, [("carryb", j) for j in range(48)], [("carryb", j) for j in range(48)])
            final = []
            if dbg and dbg.get("notile"):
                return final
            for ti in range(NT):
                final += tile(x_main, ti * T, "main", out_d)
            return final

        if not dbg:
            final = program()
            P.emit(final_waits=[final[-1]])
        else:
            tensors = dict(hT=hT, dtt=dtt, da=da, cum=cum, ecum=ecum, te=te2[0], dec=dec2[0], R1=R1, va=va, ma=ma, BT=BT2[0], CT=CT2[0],
                           xTg=xTg2[0], zsg=zsg2[0], stT=stT, R3=R3, upad=upad, U_f=U_f,
                           L_f=L_f, ident_f=ident_f, Arow=Arow, rstd=rstd, ucarry=ucarry, carryb=carryb, dahi=dahi,
                           dalo=dalo, R2=R2, stb=stb)
            try:
                program()
            except _Stop:
                pass
            col = 0
            last = None
            for name, ncols in dbg["dump"]:
                tns = tensors[name]
                flat = tns[:] if len(tns.shape) == 2 else tns[:].rearrange("p a b -> p (a b)")
                last = P.op("pool", lambda e, flat=flat, col=col, ncols=ncols: e.dma_start(out=dbg_d[:, col:col + ncols], in_=flat[:, 0:ncols]),
                            [], list(P.last_w.keys()), dma="dbg")
                col += ncols
            P.emit(final_waits=[last])
    return nc


_NC_CACHE = {}


def make_in_maps(x, norm_w, w_in, conv_a_w, conv_a_b, ln_a_w, ln_a_b, w_a_out, conv_b_w, conv_b_b, dt_bias, a_log,
           d_skip, ssm_norm_w, w_b_out, w_o, final_norm_w):
    f = lambda a: np.ascontiguousarray(np.asarray(a, dtype=np.float32))
    x = f(x)
    colmaj = lambda v: f(v).reshape(-1, 128).T
    pcol = np.zeros((128, NPC), np.float32)
    pcol[:, PC_NORMW:PC_NORMW + 16] = colmaj(norm_w[0])
    caw = f(conv_a_w[0])
    pcol[:, PC_CAW:PC_CAW + 496] = caw.reshape(31, 16, 128).transpose(2, 1, 0).reshape(128, 496)
    pcol[:, PC_CAB:PC_CAB + 16] = colmaj(conv_a_b[0])
    pcol[:, PC_LNW:PC_LNW + 16] = colmaj(ln_a_w[0])
    pcol[:, PC_LNB:PC_LNB + 16] = colmaj(ln_a_b[0])
    cbw = f(conv_b_w[0])
    pcol[:, PC_CBW:PC_CBW + 192] = cbw.reshape(4, 48, 128).transpose(2, 1, 0).reshape(128, 192)
    pcol[:, PC_CBB:PC_CBB + 48] = colmaj(conv_b_b[0])
    pcol[:, PC_SNW:PC_SNW + 32] = colmaj(ssm_norm_w[0])
    prow = np.zeros((128, 192), np.float32)
    prow[:, 0:64] = f(dt_bias[0])[None, :]
    prow[:, 64:128] = f(a_log[0])[None, :]
    prow[:, 128:192] = f(d_skip[0])[None, :]
    finw = np.ascontiguousarray(np.broadcast_to(f(final_norm_w)[None, :], (128, D)))
    wi, wa, wb, wo = f(w_in[0]), f(w_a_out[0]), f(w_b_out[0]), f(w_o[0])
    zeros = np.zeros((NTOK, D), np.float32)
    in_maps = []
    for core in range(8):
        b, half = core // 2, core % 2
        pc_ = pcol.copy()
        pc_[:, PC_FLAG] = float(half)
        in_maps.append({
            "x_main": np.ascontiguousarray(x[b, half * NTOK:(half + 1) * NTOK]),
            "x_prev": zeros if half == 0 else np.ascontiguousarray(x[b, 0:NTOK]),
            "w_in": wi, "w_a_out": wa, "w_b_out": wb, "w_o": wo,
            "pcol": pc_, "prow": prow, "finw": finw,
        })
    return in_maps


def kernel(**inputs):
    in_maps = make_in_maps(**inputs)
    if "nc" not in _NC_CACHE:
        _NC_CACHE["nc"] = build_nc()
    res = run_bass_kernel_spmd(_NC_CACHE["nc"], in_maps, core_ids=list(range(8)))
    out = np.empty((4, 2 * NTOK, D), np.float32)
    for core in range(8):
        b, half = core // 2, core % 2
        out[b, half * NTOK:(half + 1) * NTOK] = res.results[core]["out"]
    return out
```

```python
import numpy as np
from contextlib import ExitStack
import concourse.bass as bass
import concourse.mybir as mybir
from concourse.bass_utils import run_bass_kernel_spmd

F32 = mybir.dt.float32
BF16 = mybir.dt.bfloat16
AF = mybir.ActivationFunctionType
ALU = mybir.AluOpType
AX = mybir.AxisListType

D = 2048
KC = 16
T = 512
NTOK = 2048
NT = NTOK // T
A_VAL, A_GATE, A_Z, S_Z, S_X, S_B, S_C, S_DT, G_A, G_B = 0, 2048, 4096, 6144, 10240, 14336, 15360, 16384, 16448, 18496
DIN = 20544
HS = 256
NORM_EPS = 1e-6
LN_EPS = 1e-5

PC_NORMW = 0
PC_CAW = PC_NORMW + 16
PC_CAB = PC_CAW + 16 * 31
PC_LNW = PC_CAB + 16
PC_LNB = PC_LNW + 16
PC_CBW = PC_LNB + 16
PC_CBB = PC_CBW + 48 * 4
PC_SNW = PC_CBB + 48
PC_FLAG = PC_SNW + 32
NPC = PC_FLAG + 1

ENGS = ("pe", "act", "dve", "pool", "sp")


class Prog:
    LAT = 0.5
    WIN = 64

    def __init__(self, nc):
        self.nc = nc
        self.ops = []
        self.last_w = {}
        self.readers = {}
        self.last_dma = {}

    def op(self, eng, fn, reads=(), writes=(), dma=None, creads=(), cost=None):
        ex = [k for k in reads if isinstance(k, str) and k.startswith("pb")]
        if ex:
            reads = [k for k in reads if k not in ex]
            writes = list(writes) + ex
        i = len(self.ops)
        deps = {}
        for k in list(creads) + list(reads):
            w = self.last_w.get(k)
            if w is not None:
                deps[w] = True
        for k in writes:
            w = self.last_w.get(k)
            if w is not None:
                deps[w] = deps.get(w, False) or (k in ex)
            for r in self.readers.get(k, ()):
                deps.setdefault(r, False)
        odeps = set()
        if dma is not None:
            p = self.last_dma.get(dma)
            if p is not None and p not in deps:
                odeps.add(p)
            self.last_dma[dma] = i
        for k in reads:
            self.readers.setdefault(k, []).append(i)
        for k in writes:
            self.last_w[k] = i
            self.readers[k] = []
        if cost is None:
            cost = 2.0 if dma is not None else 0.3
        self.ops.append([eng, fn, deps, dma, cost, odeps])
        return i

    def schedule(self):
        ops = self.ops
        n = len(ops)
        succ = [[] for _ in range(n)]
        indeg = [0] * n
        for i, o in enumerate(ops):
            alld = set(o[2]) | o[5]
            for d in alld:
                succ[d].append(i)
            indeg[i] = len(alld)
        finish = [0.0] * n
        drt = [0.0] * n
        ready = {e: [] for e in ENGS}
        for i in range(n):
            if indeg[i] == 0:
                ready[ops[i][0]].append(i)
        free = {e: 0.0 for e in ENGS}
        order = {e: [] for e in ENGS}
        done = 0
        WIN = self.WIN
        while done < n:
            best = None
            for e in ENGS:
                r = ready[e]
                if not r:
                    continue
                f = free[e]
                for i in r[:WIN]:
                    t = drt[i] if drt[i] > f else f
                    if best is None or t < best[0] or (t == best[0] and i < best[2]):
                        best = (t, e, i)
            start, e, i = best
            ready[e].remove(i)
            o = ops[i]
            if o[3] is not None:
                free[e] = start + (6.0 if e == "pool" else 0.1)
                finish[i] = start + o[4]
            else:
                free[e] = start + o[4]
                finish[i] = free[e]
            order[e].append(i)
            done += 1
            for s_ in succ[i]:
                if i in ops[s_][5]:
                    t = start
                else:
                    t = finish[i] + (0.0 if (ops[s_][0] == e and o[3] is None) else self.LAT)
                if t > drt[s_]:
                    drt[s_] = t
                indeg[s_] -= 1
                if indeg[s_] == 0:
                    r = ready[ops[s_][0]]
                    lo, hi = 0, len(r)
                    while lo < hi:
                        mid = (lo + hi) // 2
                        if r[mid] < s_:
                            lo = mid + 1
                        else:
                            hi = mid
                    r.insert(lo, s_)
        self.makespan = max(finish) if n else 0.0
        return order

    def emit(self, final_waits=(), do_schedule=True):
        nc = self.nc
        ops = self.ops
        if do_schedule:
            order = self.schedule()
        else:
            order = {e: [i for i, o in enumerate(ops) if o[0] == e] for e in ENGS}
        wdeps = []
        needed = {e: set() for e in ENGS}
        for i, o in enumerate(ops):
            e = o[0]
            wl = []
            for d, is_raw in o[2].items():
                od = ops[d]
                if od[3] is None and od[0] == e:
                    if e in ("pe", "sp") or not (is_raw or o[3] is not None):
                        continue
                wl.append(d)
                if od[3] is None:
                    needed[od[0]].add(d)
            wdeps.append(wl)
        semv = {}
        for e in ENGS:
            c = 0
            for i in order[e]:
                if i in needed[e]:
                    c += 1
                    semv[i] = c
        dcount = {}
        dval = {}
        for e in ENGS:
            for i in order[e]:
                nm = ops[i][3]
                if nm is not None:
                    dcount[nm] = dcount.get(nm, 0) + 16
                    dval[i] = dcount[nm]
        with ExitStack() as st:
            esem = {e: st.enter_context(nc.semaphore("s_" + e)) for e in ENGS}
            dsem = {nm: st.enter_context(nc.semaphore("d_" + nm)) for nm in dcount}
            block = st.enter_context(nc.Block())

            def run(engname, e):
                seen = {}
                for i in order[engname]:
                    o = ops[i]
                    waits = {}
                    for d in wdeps[i]:
                        od = ops[d]
                        if od[3] is not None:
                            key, val = ("dma", od[3]), dval[d]
                        else:
                            key, val = ("eng", od[0]), semv[d]
                        if seen.get(key, -1) >= val:
                            continue
                        if waits.get(key, -1) < val:
                            waits[key] = val
                    for key, val in waits.items():
                        seen[key] = val
                        e.wait_ge(esem[key[1]] if key[0] == "eng" else dsem[key[1]], val)
                    ins = o[1](e)
                    if o[3] is not None:
                        ins.then_inc(dsem[o[3]], 16)
                    elif i in semv:
                        ins.then_inc(esem[engname], 1)
                if engname == "sp":
                    for i in final_waits:
                        e.wait_ge(dsem[ops[i][3]], dcount[ops[i][3]])

            @block.tensor
            def _(e):
                run("pe", e)

            @block.scalar
            def _(e):
                run("act", e)

            @block.vector
            def _(e):
                run("dve", e)

            @block.gpsimd
            def _(e):
                run("pool", e)

            @block.sync
            def _(e):
                run("sp", e)


def hsb_list():
    L = []
    for c0 in range(0, S_DT, HS):
        L.append(("w_in", 0, c0))
    for c0 in range(G_A, DIN, HS):
        L.append(("w_in", 0, c0))
    for c0 in range(0, D, HS):
        L.append(("w_a_out", 0, c0))
    for c0 in range(0, D, HS):
        L.append(("w_o", 0, c0))
    for c0 in range(0, D, HS):
        for kh in range(2):
            L.append(("w_b_out", kh * 2048, c0))
    return L


HSB = hsb_list()
NH = len(HSB)


def h_in(col):
    if col < S_DT:
        return col // HS, (col % HS) // 128
    c = col - G_A
    return S_DT // HS + c // HS, (c % HS) // 128


H_WA = S_DT // HS + (DIN - G_A) // HS
H_WO = H_WA + D // HS
H_WB = H_WO + D // HS


class _Stop(Exception):
    pass


def build_nc(dbg=None):
    nc = bass.Bass("TRN2", target_bir_lowering=False)
    dram_in = lambda n, s: nc.dram_tensor(n, s, F32, kind="ExternalInput").ap()
    x_main = dram_in("x_main", [NTOK, D])
    x_prev = dram_in("x_prev", [NTOK, D])
    W = {"w_in": dram_in("w_in", [D, DIN]), "w_a_out": dram_in("w_a_out", [D, D]),
         "w_b_out": dram_in("w_b_out", [2 * D, D]), "w_o": dram_in("w_o", [D, D])}
    pcol_d = dram_in("pcol", [128, NPC])
    prow_d = dram_in("prow", [128, 192])
    finw_d = dram_in("finw", [128, D])
    out_d = nc.dram_tensor("out", [NTOK, D], F32, kind="ExternalOutput").ap()
    wsc = nc.dram_tensor("wsc", [NH, 128, KC, HS], BF16).ap()
    dbg_d = nc.dram_tensor("dbg", [128, 16384], F32, kind="ExternalOutput").ap() if dbg else None

    P = Prog(nc)
    with ExitStack() as st:
        sbt = lambda n, s, d: st.enter_context(nc.sbuf_tensor("sb_" + n, s, d))
        pst = lambda n, s, d: st.enter_context(nc.psum_tensor(n, s, d))
        pb = [pst("pb%d" % i, [128, 512], F32) for i in range(7)]
        pb7 = pst("pb7", [128, 1024], BF16)
        hT = sbt("hT", [128, KC, T], BF16)
        NSLOT = 3
        wslot = [sbt("wslot%d" % i, [128, KC, HS], BF16) for i in range(NSLOT)]
        R1 = sbt("R1", [128, 16 * T], F32)
        va = sbt("va", [128, KC, T], BF16)
        ma = sbt("ma", [128, KC, T], BF16)
        BT2 = [sbt("BT%d" % i, [128, 2, T], BF16) for i in range(2)]
        CT2 = [sbt("CT%d" % i, [128, 2, T], BF16) for i in range(2)]
        xTg2 = [sbt("xTg%d" % i, [128, 4, T], BF16) for i in range(2)]
        zsgbuf = sbt("zsgbuf", [128, 2 * 4 * T], BF16)
        zsg2 = [zsgbuf[:, i * 4 * T:(i + 1) * 4 * T].rearrange("p (b t) -> p b t", b=4) for i in range(2)]
        te2 = [sbt("te%d" % i, [128, 256], F32) for i in range(2)]
        dec2 = [sbt("dec%d" % i, [128, 256], F32) for i in range(2)]
        stT = sbt("stT", [128, 8, 512], F32)
        stb = sbt("stb", [128, 512], BF16)
        R3 = sbt("R3", [128, 2048], F32)
        R2 = sbt("R2", [128, 2048], F32)
        upad = sbt("upad", [128, 30 + T], BF16)
        ucarry = sbt("ucarry", [128, 16, 30], BF16)
        sgt = sbt("sgt", [128, T], F32)
        sqb2 = [sbt("sqb%d" % i, [128, T], BF16) for i in range(2)]
        ucb2 = [sbt("ucb%d" % i, [128, T], BF16) for i in range(2)]
        rstd = sbt("rstd", [128, T], F32)
        saz = sbt("saz", [128, T], F32)
        xrawb = [sbt("xrawb%d" % i, [128, 4 + T], BF16) for i in range(2)]
        diag4 = [sbt("diag4_%d" % i, [128, 4, 128], BF16) for i in range(2)]
        cacc = sbt("cacc", [128, T], F32)
        carryb = sbt("carryb", [128, 48, 3], BF16)
        Zhi = sbt("Zhi", [128, 8, 128], BF16)
        Zlo = sbt("Zlo", [128, 8, 128], BF16)
        cbTm4 = sbt("cbTm4", [128, 4, 128], BF16)
        xp2 = [sbt("xp%d" % i, [128, 512], BF16) for i in range(2)]
        xpp2 = [sbt("xpp%d" % i, [128, 512], BF16) for i in range(2)]
        btm2 = [sbt("btm%d" % i, [128, 128], BF16) for i in range(2)]
        dtt = sbt("dtt", [128, 256], F32)
        da = sbt("da", [128, 256], F32)
        cum = sbt("cum", [128, 256], F32)
        ecum = sbt("ecum", [128, 256], F32)
        dahi = sbt("dahi", [128, 256], BF16)
        dalo = sbt("dalo", [128, 256], BF16)
        ssc = sbt("ssc", [128, 2], F32)
        ssc2 = sbt("ssc2", [128, 2], F32)
        ssc3 = sbt("ssc3", [128, 2], F32)
        ones_b = sbt("ones_b", [128, 128], BF16)
        dummy = sbt("dmy_t", [128, 2], F32)
        pcol = sbt("pcol", [128, NPC], F32)
        prow = sbt("prow", [128, 192], F32)
        Arow = sbt("Arow", [128, 64], F32)
        wdt = sbt("wdt", [128, KC, 64], BF16)
        ident_f = sbt("ident_f", [128, 128], F32)
        ones_f = sbt("ones_f", [128, 128], F32)
        U_f = sbt("U_f", [128, 128], F32)
        L_f = sbt("L_f", [128, 128], F32)
        ident_b = sbt("ident_b", [128, 128], BF16)
        U_b = sbt("U_b", [128, 128], BF16)
        L_b = sbt("L_b", [128, 128], BF16)

        msq = sgt
        dtx, dta, dtl, dahf = sgt[:, 0:256], sgt[:, 256:512], saz[:, 0:256], saz[:, 256:512]
        R2b = R2[:].bitcast(BF16)
        M1q = [R2b[:, 0:1024].rearrange("p (r l) -> p r l", r=8), R2b[:, 1024:2048].rearrange("p (r l) -> p r l", r=8)]
        ysb = R2[:, 1024:1536]
        xDq = [R2b[:, 3072:3584], R2b[:, 3584:4096]]
        mu = cacc
        t1 = cacc
        uconv = R1[:].rearrange("p (j t) -> p j t", j=16)
        ynT = R1[:].bitcast(BF16).rearrange("p (j t) -> p j t", j=32)
        xres = R1[:].rearrange("p (b d) -> p b d", b=4)
        xin = R3
        yg = R3[:].rearrange("p (b t) -> p b t", b=4)
        xinB = zsgbuf[:].bitcast(F32)
        ma_f = ma[:].rearrange("p a b -> p (a b)").bitcast(F32)
        finw_v, sq_v = ma_f[:, 0:2048], ma_f[:, 2048:4096]
        sqtmp = R2
        diag = R2[:].bitcast(BF16)[:, 0:31 * 128].rearrange("p (k c) -> p k c", k=31)

        C = ["pcol", "prow", "ident_f", "ones_f", "U_f", "L_f", "ident_b", "U_b", "L_b", "Arow", "wdt", "ones_b"]

        def handover(keys):
            P.op("dve", lambda e: e.tensor_copy(out=dummy[:, 0:1], in_=dummy[:, 1:2]), [], keys)

        def fsz(ap):
            n_ = 1
            for d_ in ap.shape[1:]:
                n_ *= d_
            return n_

        def mm(out, lhsT, rhs, start, stop, r, w, cr=()):
            c_ = max(0.07, fsz(rhs) / 2200.0 * (4 if rhs.dtype == F32 else 1))
            P.op("pe", lambda e: e.matmul(out, lhsT=lhsT, rhs=rhs, start=start, stop=stop), r, w, creads=cr, cost=c_)

        def tr(out, in_, ident, r, w):
            P.op("pe", lambda e: e.transpose(out=out, in_=in_, identity=ident), r, w, creads=C,
                 cost=0.25 if in_.dtype == F32 else 0.08)

        def act(out, in_, func, r, w, scale=None, bias=None, eng="act"):
            kw = {}
            if scale is not None:
                kw["scale"] = scale
            if bias is not None:
                kw["bias"] = bias
            P.op(eng, lambda e: e.activation(out=out, in_=in_, func=func, **kw), r, w, creads=C, cost=0.45 + fsz(out) / 1150.0)

        def vcost(eng, out):
            return (0.15 + fsz(out) / 900.0) if eng == "dve" else (0.3 + fsz(out) / 450.0)

        def tt(eng, out, in0, in1, op, r, w):
            P.op(eng, lambda e: e.tensor_tensor(out=out, in0=in0, in1=in1, op=op), r, w, creads=C, cost=vcost(eng, out))

        def ts(eng, out, in0, s1, s2, op0, op1, r, w):
            if op1 is None:
                P.op(eng, lambda e: e.tensor_scalar(out=out, in0=in0, scalar1=s1, scalar2=None, op0=op0), r, w, creads=C,
                     cost=vcost(eng, out))
            else:
                P.op(eng, lambda e: e.tensor_scalar(out=out, in0=in0, scalar1=s1, scalar2=s2, op0=op0, op1=op1), r, w, creads=C,
                     cost=vcost(eng, out))

        def stt(eng, out, in0, scalar, in1, op0, op1, r, w):
            P.op(eng, lambda e: e.scalar_tensor_tensor(out=out, in0=in0, scalar=scalar, in1=in1, op0=op0, op1=op1), r, w,
                 creads=C, cost=vcost(eng, out))

        def cp(eng, out, in_, r, w):
            P.op(eng, lambda e: e.tensor_copy(out=out, in_=in_), r, w, creads=C, cost=vcost(eng, out))

        def recip(ap_, r, w):
            P.op("dve", lambda e: e.reciprocal(out=ap_, in_=ap_), r, w, cost=0.15 + fsz(ap_) * 6.5 / 960.0)

        def dma(eng, out, in_, r, w, sem):
            nbytes = fsz(out) * out.shape[0] * (2 if out.dtype == BF16 else 4)
            return P.op(eng, lambda e: e.dma_start(out=out, in_=in_), r, w, dma=sem, cost=2.0 + nbytes / 200e3)

        def pc(i):
            return pcol[:, i:i + 1]

        dma("sp", pcol[:], pcol_d, [], ["pcol"], "c0")
        dma("sp", prow[:], prow_d, [], ["prow"], "c1")
        P.op("dve", lambda e: e.memset(ident_f[:], 0.0), [], ["ident_f"])
        P.op("dve", lambda e: e.memset(ones_f[:], 1.0), [], ["ones_f"])
        P.op("pool", lambda e: e.affine_select(out=ident_f[:], in_=ident_f[:], compare_op=ALU.not_equal, fill=1.0,
                                               base=0, pattern=[[-1, 128]], channel_multiplier=1), ["ident_f"], ["ident_f"])
        P.op("pool", lambda e: e.affine_select(out=U_f[:], in_=ones_f[:], compare_op=ALU.is_ge, fill=0.0,
                                               base=0, pattern=[[1, 128]], channel_multiplier=-1), ["ones_f"], ["U_f"])
        P.op("pool", lambda e: e.affine_select(out=L_f[:], in_=ones_f[:], compare_op=ALU.is_ge, fill=0.0,
                                               base=-1, pattern=[[-1, 128]], channel_multiplier=1), ["ones_f"], ["L_f"])
        cp("dve", ident_b[:], ident_f[:], ["ident_f"], ["ident_b"])
        cp("dve", U_b[:], U_f[:], ["U_f"], ["U_b"])
        cp("dve", L_b[:], L_f[:], ["L_f"], ["L_b"])
        cp("dve", ones_b[:], ones_f[:], ["ones_f"], ["ones_b"])
        P.op("dve", lambda e: e.memset(stT[:], 0.0), [], ["stT"])
        P.op("dve", lambda e: e.memset(ucarry[:], 0.0), [], [("ucarry", j) for j in range(16)])
        P.op("dve", lambda e: e.memset(carryb[:], 0.0), [], [("carryb", j) for j in range(48)])
        act(Arow[:], prow[:, 64:128], AF.Exp, ["prow"], ["Arow"])
        ts("dve", Arow[:], Arow[:], -1.0, None, ALU.mult, None, ["Arow"], ["Arow"])
        P.op("dve", lambda e: e.memset(dummy[:], 0.0), [], [])
        dma("pool", wdt[:], W["w_in"][:, S_DT:S_DT + 64].rearrange("(kc p) c -> p kc c", p=128), [], ["wdt"], "cwdt")

        def cast_h(h):
            name, k0, c0 = HSB[h]
            src = W[name][k0:k0 + 2048, c0:c0 + HS].rearrange("(kc p) c -> p kc c", p=128)
            dma("pool", wsc[h], src, [], [("wsc", h)], "cast%d" % (h % 56))

        order = []
        xh = [h_in(S_X + i * HS)[0] for i in range(16)]
        bh = [h_in(S_B + i * HS)[0] for i in range(4)]
        for g in range(8):
            if g % 2 == 0:
                order.append(bh[g // 2])
            order += xh[2 * g:2 * g + 2]
        order += [h_in(A_GATE + i * HS)[0] for i in range(8)]
        order += [h_in(A_VAL + i * HS)[0] for i in range(8)]
        for h in range(NH):
            if h not in order:
                order.append(h)
        if dbg and "ncast" in dbg:
            order = order[:dbg["ncast"]]
        for h in order:
            cast_h(h)

        wctr = [0]
        xbc_ctr = [0]

        def getw(h):
            s = wctr[0] % NSLOT
            wctr[0] += 1
            dma("sp", wslot[s][:], wsc[h], [("wsc", h)], [("ws", s)], "wl%d" % s)
            return wslot[s], ("ws", s)

        bank_rr = [0]

        def inproj(wt, wk, cb, bank, t_lo=0):
            for kc in range(KC):
                mm(pb[bank][:, t_lo:T], wt[:, kc, cb * 128:(cb + 1) * 128], hT[:, kc, t_lo:T], kc == 0, kc == KC - 1,
                   [wk, "hT"], ["pb%d" % bank])

        banks = [[0, 1]]
        tctr = [0]

        def setbanks(l):
            banks[0] = [0, 1]

        def nextbank():
            b = banks[0][bank_rr[0] % len(banks[0])]
            bank_rr[0] += 1
            return b

        def stage_end(name, mode):
            if dbg and dbg.get("stop") == name and dbg.get("mode") == mode:
                raise _Stop()

        def tile(x_src, t0, mode, out_ap=None):
            main = mode == "main"
            tctr[0] += 1
            te, dec = te2[tctr[0] % 2], dec2[tctr[0] % 2]
            kte, kdec = ("te", tctr[0] % 2), ("dec", tctr[0] % 2)
            setbanks([0, 1])
            handover([("zsg", 0), ("zsg", 1), "xinB"])
            for tb in range(4):
                xin_, kxin = (R3[:], "R3") if tb % 2 == 0 else (xinB, "xinB")
                sc_, ksc = (ssc, "ssc") if tb % 2 == 0 else (ssc3, "ssc3")
                dma("sp", xin_, x_src[t0 + tb * 128:t0 + (tb + 1) * 128, :], [], [kxin], "xin%d" % (tb % 2))
                act(sqtmp[:], xin_, AF.Square, [kxin], ["R2"])
                P.op("dve", lambda e, sc_=sc_: e.reduce_sum(out=sc_[:, 0:1], in_=sqtmp[:], axis=AX.X), ["R2"], [ksc], cost=2.4)
                ts("dve", sc_[:, 1:2], sc_[:, 0:1], 1.0 / D, NORM_EPS, ALU.mult, ALU.add, [ksc], [ksc])
                act(sc_[:, 1:2], sc_[:, 1:2], AF.Sqrt, [ksc], [ksc])
                recip(sc_[:, 1:2], [ksc], [ksc])
                ts("dve", xin_, xin_, sc_[:, 1:2], None, ALU.mult, None, [kxin, ksc], [kxin])
                for kc in range(KC):
                    q = kc // 4
                    tr(pb[2 + q][:, (kc % 4) * 128:(kc % 4 + 1) * 128], xin_[:, kc * 128:(kc + 1) * 128], ident_f[:],
                       [kxin], ["pb%d" % (2 + q)])
                for kc in range(KC):
                    q = kc // 4
                    act(hT[:, kc, tb * 128:(tb + 1) * 128], pb[2 + q][:, (kc % 4) * 128:(kc % 4 + 1) * 128], AF.Copy,
                        ["pb%d" % (2 + q)], ["hT"], scale=pc(PC_NORMW + kc))

            handover([("zsg", 0), ("zsg", 1), "xinB"])
            stage_end("H", mode)
            for c in range(4):
                for kc in range(KC):
                    mm(pb[2][:, c * 64:(c + 1) * 64], hT[:, kc, c * 128:(c + 1) * 128], wdt[:, kc, :], kc == 0, kc == KC - 1,
                       ["hT"], ["pb2"], cr=["wdt"])
            v3 = lambda a: a.rearrange("p (c r) -> p c r", c=4)
            tt("dve", v3(dtx[:]), v3(pb[2][:, 0:256]), prow[:, 0:64].unsqueeze(1).to_broadcast([128, 4, 64]), ALU.add,
               ["pb2"], ["sgt"])
            ts("dve", dtl[:], dtx[:], 0.0, None, ALU.max, None, ["sgt"], ["saz"])
            stt("dve", dta[:], dtl[:], -2.0, dtx[:], ALU.mult, ALU.add, ["sgt", "saz"], ["sgt"])
            act(dta[:], dta[:], AF.Exp, ["sgt"], ["sgt"])
            ts("dve", dta[:], dta[:], 1.0, None, ALU.add, None, ["sgt"], ["sgt"])
            act(dta[:], dta[:], AF.Ln, ["sgt"], ["sgt"])
            tt("dve", dtt[:], dtl[:], dta[:], ALU.add, ["sgt", "saz"], ["dtt"])
            tt("dve", v3(da[:]), v3(dtt[:]), Arow[:].unsqueeze(1).to_broadcast([128, 4, 64]), ALU.mult, ["dtt"], ["da"])
            mm(pb[3][:, 0:256], U_f[:], da[:], True, True, ["da"], ["pb3"], cr=C)
            mm(pb[4][:, 0:256], ones_f[:], da[:], True, True, ["da"], ["pb4"], cr=C)
            cp("dve", cum[:], pb[3][:, 0:256], ["pb3"], ["cum"])
            act(ecum[:], cum[:], AF.Exp, ["cum"], ["ecum"])
            tt("dve", te[:], pb[4][:, 0:256], cum[:], ALU.subtract, ["pb4", "cum"], [kte])
            act(te[:], te[:], AF.Exp, [kte], [kte])
            tt("dve", te[:], te[:], dtt[:], ALU.mult, [kte, "dtt"], [kte])
            act(dec[:], pb[4][:, 0:256], AF.Exp, ["pb4"], [kdec])
            cp("dve", dahi[:], da[:], ["da"], ["dahi"])
            cp("dve", dahf[:], dahi[:], ["dahi"], ["saz"])
            tt("dve", dalo[:], da[:], dahf[:], ALU.subtract, ["da", "saz"], ["dalo"])

            stage_end("DT", mode)
            setbanks([0, 1, 3, 4] if main else [0, 1])
            def a1_inproj(j):
                if j % 2 == 0:
                    a1_inproj.wg = getw(h_in(A_GATE + j * 128)[0])
                    a1_inproj.wv = getw(h_in(A_VAL + j * 128)[0])
                cb = j % 2
                lo = 0 if main else T - 128
                sg_, ksg = (sgt, "sgt") if j % 2 == 0 else (saz, "saz")
                b0 = nextbank()
                inproj(a1_inproj.wg[0], a1_inproj.wg[1], cb, b0, lo)
                act(sg_[:, lo:T], pb[b0][:, lo:T], AF.Sigmoid, ["pb%d" % b0], [ksg])
                b1 = b0
                inproj(a1_inproj.wv[0], a1_inproj.wv[1], cb, b1, lo)
                if main:
                    cp("dve", upad[:, 0:30], ucarry[:, j, :], [("ucarry", j)], ["upad"])
                tt("dve", upad[:, 30 + lo:30 + T], pb[b1][:, lo:T], sg_[:, lo:T], ALU.mult, ["pb%d" % b1, ksg], ["upad"])
                cp("dve", ucarry[:, j, :], upad[:, T:T + 30], ["upad"], [("ucarry", j)])

            def a1_conv(j):
                P.op("dve", lambda e: e.tensor_tensor(
                    out=diag, in0=ident_b[:].unsqueeze(1).to_broadcast([128, 31, 128]),
                    in1=pcol[:, PC_CAW + j * 31:PC_CAW + (j + 1) * 31].unsqueeze(2).to_broadcast([128, 31, 128]),
                    op=ALU.mult), [], ["R2"], creads=C, cost=4.4)
                for k in range(31):
                    mm(pb[2][:], diag[:, k, :], upad[:, k:k + T], k == 0, k == 30, ["R2", "upad"], ["pb2"])
                act(uconv[:, j, :], pb[2][:], AF.Identity, ["pb2"], [("R1", j)], bias=pc(PC_CAB + j))
                act(ucb2[j % 2][:], pb[2][:], AF.Identity, ["pb2"], [("ucb", j % 2)], bias=pc(PC_CAB + j))
                act(sqb2[j % 2][:], uconv[:, j, :], AF.Square, [("R1", j)], [("sqb", j % 2)])
                mm(pb[5][:], ones_b[:], ucb2[j % 2][:], j == 0, j == 15, [("ucb", j % 2)], ["pb5"], cr=C)
                mm(pb[6][:], ones_b[:], sqb2[j % 2][:], j == 0, j == 15, [("sqb", j % 2)], ["pb6"], cr=C)

            if main or mode == "pre_last":
                for j in range(16):
                    a1_inproj(j)
                    if main:
                        a1_conv(j)
            stage_end("A1", mode)
            if main:
                ts("dve", mu[:], pb[5][:], 1.0 / D, None, ALU.mult, None, ["pb5"], ["cacc"])
                tt("dve", msq[:], mu[:], mu[:], ALU.mult, ["cacc"], ["sgt"])
                stt("dve", rstd[:], pb[6][:], 1.0 / D, msq[:], ALU.mult, ALU.subtract, ["pb6", "sgt"], ["rstd"])
                ts("dve", rstd[:], rstd[:], LN_EPS, None, ALU.add, None, ["rstd"], ["rstd"])
                act(rstd[:], rstd[:], AF.Sqrt, ["rstd"], ["rstd"])
                recip(rstd[:], ["rstd"], ["rstd"])
                for j in range(16):
                    if j % 2 == 0:
                        wz = getw(h_in(A_Z + j * 128)[0])
                    b0 = nextbank()
                    inproj(wz[0], wz[1], j % 2, b0)
                    act(saz[:], pb[b0][:], AF.Silu, ["pb%d" % b0], ["saz"])
                    tt("dve", uconv[:, j, :], uconv[:, j, :], mu[:], ALU.subtract, [("R1", j), "cacc"], [("R1", j)])
                    tt("dve", uconv[:, j, :], uconv[:, j, :], rstd[:], ALU.mult, [("R1", j), "rstd"], [("R1", j)])
                    act(uconv[:, j, :], uconv[:, j, :], AF.Silu, [("R1", j)], [("R1", j)], scale=pc(PC_LNW + j), bias=pc(PC_LNB + j))
                    tt("dve", va[:, j, :], uconv[:, j, :], saz[:], ALU.mult, [("R1", j), "saz"], [("va", j)])
                stage_end("A2", mode)
                for jo in range(16):
                    if jo % 2 == 0:
                        wa = getw(H_WA + jo // 2)
                        wg = getw(h_in(G_A + jo * 128)[0])
                    cb = jo % 2
                    b0 = nextbank()
                    inproj(wg[0], wg[1], cb, b0)
                    act(sgt[:], pb[b0][:], AF.Sigmoid, ["pb%d" % b0], ["sgt"])
                    b1 = nextbank()
                    for kc in range(KC):
                        mm(pb[b1][:], wa[0][:, kc, cb * 128:(cb + 1) * 128], va[:, kc, :], kc == 0, kc == KC - 1,
                           [wa[1]] + [("va", kc)], ["pb%d" % b1])
                    tt("dve", ma[:, jo, :], pb[b1][:], sgt[:], ALU.mult, ["pb%d" % b1, "sgt"], [("ma", jo)])

            stage_end("A3", mode)
            setbanks([0, 1] if main else [0, 1, 2, 3, 5, 6])
            def xbc_block(wt, wk, cb, blk, dst, dkey):
                q = xbc_ctr[0] % 2
                xbc_ctr[0] += 1
                b0 = nextbank()
                inproj(wt, wk, cb, b0)
                cp("dve", xrawb[q][:, 0:3], carryb[:, blk, :], [("carryb", blk)], [("xraw", q)])
                act(xrawb[q][:, 3:3 + T], pb[b0][:], AF.Copy, ["pb%d" % b0], [("xraw", q)])
                cp("dve", carryb[:, blk, :], xrawb[q][:, T:T + 3], [("xraw", q)], [("carryb", blk)])
                P.op("dve", lambda e: e.tensor_tensor(
                    out=diag4[q][:], in0=ident_b[:].unsqueeze(1).to_broadcast([128, 4, 128]),
                    in1=pcol[:, PC_CBW + blk * 4:PC_CBW + blk * 4 + 4].unsqueeze(2).to_broadcast([128, 4, 128]),
                    op=ALU.mult), [], [("diag4", q)], creads=C, cost=0.7)
                b1 = b0
                for k in range(4):
                    mm(pb[b1][:], diag4[q][:, k, :], xrawb[q][:, k:k + T], k == 0, k == 3, [("diag4", q), ("xraw", q)], ["pb%d" % b1])
                act(dst, pb[b1][:], AF.Silu, ["pb%d" % b1], [dkey], bias=pc(PC_CBB + blk))

            pre_last = mode == "pre_last"

            def prep_list(g):
                L = []
                par, pp = g % 2, (g // 2) % 2
                hold = {}
                if par == 0:
                    def fB(cb):
                        if cb == 0:
                            hold["B"] = getw(h_in(S_B + g * 128)[0])
                        xbc_block(hold["B"][0], hold["B"][1], cb, 32 + g + cb, BT2[pp][:, cb, :], ("BT", pp))
                    L += [lambda cb=cb: fB(cb) for cb in range(2)]
                    if main or pre_last:
                        def fC(cb):
                            if cb == 0:
                                hold["C"] = getw(h_in(S_C + g * 128)[0])
                            if main:
                                xbc_block(hold["C"][0], hold["C"][1], cb, 40 + g + cb, CT2[pp][:, cb, :], ("CT", pp))
                            else:
                                b0 = nextbank()
                                inproj(hold["C"][0], hold["C"][1], cb, b0, T - 128)
                                act(carryb[:, 40 + g + cb, :], pb[b0][:, T - 3:T], AF.Copy, ["pb%d" % b0], [("carryb", 40 + g + cb)])
                        L += [lambda cb=cb: fC(cb) for cb in range(2)]

                def fx(b):
                    if b % 2 == 0:
                        hold["x"] = getw(h_in(S_X + (4 * g + b) * 128)[0])
                    xbc_block(hold["x"][0], hold["x"][1], b % 2, 4 * g + b, xTg2[par][:, b, :], ("xTg", par))
                L += [lambda b=b: fx(b) for b in range(4)]
                if main:
                    def fz(b):
                        if b % 2 == 0:
                            hold["z"] = getw(h_in(S_Z + (4 * g + b) * 128)[0])
                        b0 = nextbank()
                        inproj(hold["z"][0], hold["z"][1], b % 2, b0)
                        act(zsg2[par][:, b, :], pb[b0][:], AF.Silu, ["pb%d" % b0], [("zsg", par)])
                    L += [lambda b=b: fz(b) for b in range(4)]
                return L

            for f in prep_list(0):
                f()
            if main:
                handover(["R2", "M1a", "M1b", "ysb", "xDa", "xDb"])
            h3 = lambda a: a.rearrange("p (r q) -> p r q", r=8)
            bc_p = lambda a: a.unsqueeze(2).to_broadcast([128, 8, 64])
            M1k, xDk = ["M1a", "M1b"], ["xDa", "xDb"]

            def grp(g):
                par, pp = g % 2, (g // 2) % 2
                return dict(gl=g % 2, xTg=xTg2[par], zsg=zsg2[par], BT=BT2[pp], CT=CT2[pp],
                            kx=("xTg", par), kz=("zsg", par), kB=("BT", pp), kC=("CT", pp))

            def cbT4(g):
                G = grp(g)
                for c in range(4):
                    cs = slice(c * 128, (c + 1) * 128)
                    mm(pb[4][:, cs], G["BT"][:, G["gl"], cs], G["CT"][:, G["gl"], cs], True, True, [G["kB"], G["kC"]], ["pb4"])
                tt("dve", cbTm4[:], pb[4][:].rearrange("p (c l) -> p c l", c=4), U_f[:].unsqueeze(1).to_broadcast([128, 4, 128]),
                   ALU.mult, ["pb4"], ["cbTm4"])

            def S1(g, c, q):
                G = grp(g)
                cs = slice(c * 128, (c + 1) * 128)
                rs = slice(c * 64 + g * 8, c * 64 + g * 8 + 8)
                for b in range(4):
                    tr(pb7[:, b * 128:(b + 1) * 128], G["xTg"][:, b, cs], ident_b[:], [G["kx"]], ["pb7"])
                tr(pb7[:, 512:640], G["BT"][:, G["gl"], cs], ident_b[:], [G["kB"]], ["pb7"])
                cp("dve", btm2[q][:], pb7[:, 512:640], ["pb7"], [("btm", q)])
                tt("dve", h3(xpp2[q][:]), h3(pb7[:, 0:512]), bc_p(te[:, rs]), ALU.mult, ["pb7", kte], [("xpp", q)])
                if not main:
                    return
                tt("dve", h3(xp2[q][:]), h3(pb7[:, 0:512]), bc_p(dtt[:, rs]), ALU.mult, ["pb7", "dtt"], [("xp", q)])
                tt("dve", h3(xDq[q]), h3(pb7[:, 0:512]), bc_p(prow[:, 128 + g * 8:128 + g * 8 + 8]), ALU.mult,
                   ["pb7"], [xDk[q]])
                ub = U_b[:].unsqueeze(1).to_broadcast([128, 8, 128])
                tt("pool", Zhi[:], ub, dahi[:, rs].unsqueeze(2).to_broadcast([128, 8, 128]), ALU.mult, ["dahi"], ["Zhi"])
                for hh in range(2):
                    zs_ = slice(hh * 4, hh * 4 + 4)
                    mm(pb[5 + hh][:], L_b[:], Zhi[:, zs_, :].rearrange("p r l -> p (r l)"), True, True, ["Zhi"], ["pb%d" % (5 + hh)], cr=C)
                    act(M1q[q][:, zs_, :].rearrange("p r l -> p (r l)"), pb[5 + hh][:], AF.Exp, ["pb%d" % (5 + hh)], [M1k[q]])
                tt("dve", M1q[q], M1q[q], cbTm4[:, c, :].unsqueeze(1).to_broadcast([128, 8, 128]), ALU.mult,
                   [M1k[q], "cbTm4"], [M1k[q]])

            seq = [(g, c) for g in range(8) for c in range(4)]
            nxt = prep_list(1)

            def fill(n=1):
                for _ in range(n):
                    if nxt:
                        nxt.pop(0)()

            if main:
                cbT4(0)
            S1(0, 0, 0)
            for n, (g, c) in enumerate(seq):
                q = n % 2
                G = grp(g)
                cs = slice(c * 128, (c + 1) * 128)
                rs = slice(c * 64 + g * 8, c * 64 + g * 8 + 8)
                if main:
                    mm(pb[2][:], ident_b[:], xDq[q], True, False, [xDk[q]], ["pb2"], cr=C)
                    for r in range(8):
                        mm(pb[2][:, r * 64:(r + 1) * 64], M1q[q][:, r, :], xp2[q][:, r * 64:(r + 1) * 64], False, r == 7,
                           [M1k[q], ("xp", q)], ["pb2"])
                    act(stb[:], stT[:, g, :], AF.Copy, [("stT", g)], ["stb"])
                    mm(pb[3][:], G["CT"][:, G["gl"], cs], stb[:], True, True, [G["kC"], "stb"], ["pb3"])
                mm(pb[4][:], btm2[q][:], xpp2[q][:], True, True, [("btm", q), ("xpp", q)], ["pb4"])
                if main:
                    tt("dve", h3(ysb), h3(pb[3][:]), bc_p(ecum[:, rs]), ALU.mult, ["pb3", "ecum"], ["ysb"])
                    tt("dve", ysb, ysb, pb[2][:], ALU.add, ["ysb", "pb2"], ["ysb"])
                tt("dve", h3(stT[:, g, :]), h3(stT[:, g, :]), bc_p(dec[:, rs]), ALU.mult, [("stT", g), kdec], [("stT", g)])
                tt("dve", stT[:, g, :], stT[:, g, :], pb[4][:], ALU.add, [("stT", g), "pb4"], [("stT", g)])
                fill(1 if main else (2 if c % 2 == 0 else 1))
                if main:
                    for b in range(4):
                        tr(pb[3][:, b * 128:(b + 1) * 128], ysb[:, b * 128:(b + 1) * 128], ident_f[:], ["ysb"], ["pb3"])
                    tt("dve", yg[:, :, cs], pb[3][:].rearrange("p (b l) -> p b l", b=4), G["zsg"][:, :, cs], ALU.mult,
                       ["pb3", G["kz"]], ["R3"])
                if c == 3:
                    while nxt:
                        fill(1)
                    if main:
                        for b in range(4):
                            act(sqb2[b % 2][:], yg[:, b, :], AF.Square, ["R3"], [("sqb", b % 2)])
                            mm(pb[4][:], ones_b[:], sqb2[b % 2][:], b == 0, b == 3, [("sqb", b % 2)], ["pb4"], cr=C)
                        ts("dve", rstd[:], pb[4][:], 1.0 / 512, NORM_EPS, ALU.mult, ALU.add, ["pb4"], ["rstd"])
                        act(rstd[:], rstd[:], AF.Sqrt, ["rstd"], ["rstd"])
                        recip(rstd[:], ["rstd"], ["rstd"])
                        for b in range(4):
                            if g == 0 and b == 0:
                                handover([("R1", j) for j in range(16)] + [("yn", k) for k in range(32)])
                            stt("dve", ynT[:, 4 * g + b, :], yg[:, b, :], pc(PC_SNW + 4 * g + b), rstd[:], ALU.mult, ALU.mult,
                                ["R3", "rstd"], [("yn", 4 * g + b)])
                    if g < 7:
                        nxt.extend(prep_list(g + 2) if g + 2 < 8 else [])
                        if main:
                            cbT4(g + 1)
                if n + 1 < len(seq):
                    S1(seq[n + 1][0], seq[n + 1][1], (n + 1) % 2)
                fill(1 if main else 0)
            while nxt:
                fill(1)
            if main:
                handover(["R2", "M1a", "M1b", "ysb", "xDa", "xDb"])
            stage_end("B", mode)
            if not main:
                return
            setbanks([0, 1, 2, 3, 4, 5, 6])
            for jo in range(16):
                if jo % 2 == 0:
                    wb0 = getw(H_WB + (jo // 2) * 2)
                    wb1 = getw(H_WB + (jo // 2) * 2 + 1)
                    wg = getw(h_in(G_B + jo * 128)[0])
                cb = jo % 2
                b0 = nextbank()
                inproj(wg[0], wg[1], cb, b0)
                act(sgt[:], pb[b0][:], AF.Sigmoid, ["pb%d" % b0], ["sgt"])
                b1 = nextbank()
                for kc in range(32):
                    wb_ = wb0 if kc < 16 else wb1
                    mm(pb[b1][:], wb_[0][:, kc % 16, cb * 128:(cb + 1) * 128], ynT[:, kc, :], kc == 0, kc == 31,
                       [wb_[1], ("yn", kc)], ["pb%d" % b1])
                tt("dve", saz[:], pb[b1][:], sgt[:], ALU.mult, ["pb%d" % b1, "sgt"], ["saz"])
                tt("dve", va[:, jo, :], saz[:], ma[:, jo, :], ALU.add, ["saz", ("ma", jo)], [("va", jo)])
            stage_end("M", mode)
            handover([("yn", k) for k in range(32)] + [("xr", tb) for tb in range(4)])
            for tb in range(4):
                dma("sp", xres[:, tb, :], x_src[t0 + tb * 128:t0 + (tb + 1) * 128, :], [], [("xr", tb)], "xr%d" % tb)
            handover([("ma", j) for j in range(16)] + ["maF", "maS"])
            dma("sp", finw_v, finw_d, [], ["maF"], "finw")
            for hq in range(8):
                wo = getw(H_WO + hq)
                for tb in range(4):
                    b0 = nextbank()
                    for kc in range(KC):
                        mm(pb[b0][:, 0:HS], va[:, kc, tb * 128:(tb + 1) * 128], wo[0][:, kc, :], kc == 0, kc == KC - 1,
                           [wo[1], ("va", kc)], ["pb%d" % b0])
                    tt("dve", xres[:, tb, hq * HS:(hq + 1) * HS], xres[:, tb, hq * HS:(hq + 1) * HS], pb[b0][:, 0:HS], ALU.add,
                       [("xr", tb), "pb%d" % b0], [("xr", tb)])
            toks = []
            for tb in range(4):
                act(sq_v, xres[:, tb, :], AF.Square, [("xr", tb)], ["maS"])
                P.op("dve", lambda e: e.reduce_sum(out=ssc2[:, 0:1], in_=sq_v, axis=AX.X), ["maS"], ["ssc2"], cost=2.4)
                ts("dve", ssc2[:, 1:2], ssc2[:, 0:1], 1.0 / D, NORM_EPS, ALU.mult, ALU.add, ["ssc2"], ["ssc2"])
                act(ssc2[:, 1:2], ssc2[:, 1:2], AF.Sqrt, ["ssc2"], ["ssc2"])
                recip(ssc2[:, 1:2], ["ssc2"], ["ssc2"])
                stt("dve", xres[:, tb, :], xres[:, tb, :], ssc2[:, 1:2], finw_v, ALU.mult, ALU.mult,
                    [("xr", tb), "ssc2", "maF"], [("xr", tb)])
                toks.append(dma("sp", out_ap[t0 + tb * 128:t0 + (tb + 1) * 128, :], xres[:, tb, :], [("xr", tb)], [], "out"))
            handover([("xr", tb) for tb in range(4)] + [("R1", j) for j in range(16)])
            handover([("ma", j) for j in range(16)] + ["maF", "maS"])
            return toks

        def program():
            if not (dbg and dbg.get("nopre")):
                for ti in range(NT):
                    tile(x_prev, ti * T, "pre_last" if ti == NT - 1 else "pre")
                for g in range(8):
                    ts("dve", stT[:, g, :], stT[:, g, :], pc(PC_FLAG), None, ALU.mult, None, [("stT", g)], [("stT", g)])
                ts("dve", ucarry[:].rearrange("p j t -> p (j t)"), ucarry[:].rearrange("p j t -> p (j t)"), pc(PC_FLAG), None,
                   ALU.mult, None, [("ucarry", j) for j in range(16)], [("ucarry", j) for j in range(16)])
                ts("dve", carryb[:].rearrange("p j t -> p (j t)"), carryb[:].rearrange("p j t -> p (j t)"), pc(PC_FLAG), None,
                   ALU.mult, None, [("carryb", j) for j in range(48)], [("carryb", j) for j in range(48)])
            final = []
            if dbg and dbg.get("notile"):
                return final
            for ti in range(NT):
                final += tile(x_main, ti * T, "main", out_d)
            return final

        if not dbg:
            final = program()
            P.emit(final_waits=[final[-1]])
        else:
            tensors = dict(hT=hT, dtt=dtt, da=da, cum=cum, ecum=ecum, te=te2[0], dec=dec2[0], R1=R1, va=va, ma=ma, BT=BT2[0], CT=CT2[0],
                           xTg=xTg2[0], zsg=zsg2[0], stT=stT, R3=R3, upad=upad, U_f=U_f,
                           L_f=L_f, ident_f=ident_f, Arow=Arow, rstd=rstd, ucarry=ucarry, carryb=carryb, dahi=dahi,
                           dalo=dalo, R2=R2, stb=stb)
            try:
                program()
            except _Stop:
                pass
            col = 0
            last = None
            for name, ncols in dbg["dump"]:
                tns = tensors[name]
                flat = tns[:] if len(tns.shape) == 2 else tns[:].rearrange("p a b -> p (a b)")
                last = P.op("pool", lambda e, flat=flat, col=col, ncols=ncols: e.dma_start(out=dbg_d[:, col:col + ncols], in_=flat[:, 0:ncols]),
                            [], list(P.last_w.keys()), dma="dbg")
                col += ncols
            P.emit(final_waits=[last])
    return nc


_NC_CACHE = {}


def make_in_maps(x, norm_w, w_in, conv_a_w, conv_a_b, ln_a_w, ln_a_b, w_a_out, conv_b_w, conv_b_b, dt_bias, a_log,
           d_skip, ssm_norm_w, w_b_out, w_o, final_norm_w):
    f = lambda a: np.ascontiguousarray(np.asarray(a, dtype=np.float32))
    x = f(x)
    colmaj = lambda v: f(v).reshape(-1, 128).T
    pcol = np.zeros((128, NPC), np.float32)
    pcol[:, PC_NORMW:PC_NORMW + 16] = colmaj(norm_w[0])
    caw = f(conv_a_w[0])
    pcol[:, PC_CAW:PC_CAW + 496] = caw.reshape(31, 16, 128).transpose(2, 1, 0).reshape(128, 496)
    pcol[:, PC_CAB:PC_CAB + 16] = colmaj(conv_a_b[0])
    pcol[:, PC_LNW:PC_LNW + 16] = colmaj(ln_a_w[0])
    pcol[:, PC_LNB:PC_LNB + 16] = colmaj(ln_a_b[0])
    cbw = f(conv_b_w[0])
    pcol[:, PC_CBW:PC_CBW + 192] = cbw.reshape(4, 48, 128).transpose(2, 1, 0).reshape(128, 192)
    pcol[:, PC_CBB:PC_CBB + 48] = colmaj(conv_b_b[0])
    pcol[:, PC_SNW:PC_SNW + 32] = colmaj(ssm_norm_w[0])
    prow = np.zeros((128, 192), np.float32)
    prow[:, 0:64] = f(dt_bias[0])[None, :]
    prow[:, 64:128] = f(a_log[0])[None, :]
    prow[:, 128:192] = f(d_skip[0])[None, :]
    finw = np.ascontiguousarray(np.broadcast_to(f(final_norm_w)[None, :], (128, D)))
    wi, wa, wb, wo = f(w_in[0]), f(w_a_out[0]), f(w_b_out[0]), f(w_o[0])
    zeros = np.zeros((NTOK, D), np.float32)
    in_maps = []
    for core in range(8):
        b, half = core // 2, core % 2
        pc_ = pcol.copy()
        pc_[:, PC_FLAG] = float(half)
        in_maps.append({
            "x_main": np.ascontiguousarray(x[b, half * NTOK:(half + 1) * NTOK]),
            "x_prev": zeros if half == 0 else np.ascontiguousarray(x[b, 0:NTOK]),
            "w_in": wi, "w_a_out": wa, "w_b_out": wb, "w_o": wo,
            "pcol": pc_, "prow": prow, "finw": finw,
        })
    return in_maps


def kernel(**inputs):
    in_maps = make_in_maps(**inputs)
    if "nc" not in _NC_CACHE:
        _NC_CACHE["nc"] = build_nc()
    res = run_bass_kernel_spmd(_NC_CACHE["nc"], in_maps, core_ids=list(range(8)))
    out = np.empty((4, 2 * NTOK, D), np.float32)
    for core in range(8):
        b, half = core // 2, core % 2
        out[b, half * NTOK:(half + 1) * NTOK] = res.results[core]["out"]
    return out
```

```python
import numpy as np
from contextlib import ExitStack
import concourse.bass as bass
import concourse.mybir as mybir
from concourse.bass_utils import run_bass_kernel_spmd

F32 = mybir.dt.float32
BF16 = mybir.dt.bfloat16
AF = mybir.ActivationFunctionType
ALU = mybir.AluOpType
AX = mybir.AxisListType

D = 2048
KC = 16
T = 512
NTOK = 2048
NT = NTOK // T
A_VAL, A_GATE, A_Z, S_Z, S_X, S_B, S_C, S_DT, G_A, G_B = 0, 2048, 4096, 6144, 10240, 14336, 15360, 16384, 16448, 18496
DIN = 20544
HS = 256
NORM_EPS = 1e-6
LN_EPS = 1e-5

PC_NORMW = 0
PC_CAW = PC_NORMW + 16
PC_CAB = PC_CAW + 16 * 31
PC_LNW = PC_CAB + 16
PC_LNB = PC_LNW + 16
PC_CBW = PC_LNB + 16
PC_CBB = PC_CBW + 48 * 4
PC_SNW = PC_CBB + 48
PC_FLAG = PC_SNW + 32
NPC = PC_FLAG + 1

ENGS = ("pe", "act", "dve", "pool", "sp")


class Prog:
    LAT = 0.5
    WIN = 64

    def __init__(self, nc):
        self.nc = nc
        self.ops = []
        self.last_w = {}
        self.readers = {}
        self.last_dma = {}

    def op(self, eng, fn, reads=(), writes=(), dma=None, creads=(), cost=None, tbl=None):
        ex = [k for k in reads if isinstance(k, str) and k.startswith("pb")]
        if ex:
            reads = [k for k in reads if k not in ex]
            writes = list(writes) + ex
        i = len(self.ops)
        deps = {}
        for k in list(creads) + list(reads):
            w = self.last_w.get(k)
            if w is not None:
                deps[w] = True
        for k in writes:
            w = self.last_w.get(k)
            if w is not None:
                deps[w] = deps.get(w, False) or (k in ex)
            for r in self.readers.get(k, ()):
                deps.setdefault(r, False)
        odeps = set()
        if dma is not None:
            p = self.last_dma.get(dma)
            if p is not None and p not in deps:
                odeps.add(p)
            self.last_dma[dma] = i
        for k in reads:
            self.readers.setdefault(k, []).append(i)
        for k in writes:
            self.last_w[k] = i
            self.readers[k] = []
        if cost is None:
            cost = 2.0 if dma is not None else 0.3
        self.ops.append([eng, fn, deps, dma, cost, odeps, tbl])
        return i

    def schedule(self):
        ops = self.ops
        n = len(ops)
        succ = [[] for _ in range(n)]
        indeg = [0] * n
        for i, o in enumerate(ops):
            alld = set(o[2]) | o[5]
            for d in alld:
                succ[d].append(i)
            indeg[i] = len(alld)
        finish = [0.0] * n
        drt = [0.0] * n
        ready = {e: [] for e in ENGS}
        for i in range(n):
            if indeg[i] == 0:
                ready[ops[i][0]].append(i)
        free = {e: 0.0 for e in ENGS}
        order = {e: [] for e in ENGS}
        done = 0
        WIN = self.WIN
        cur_tbl = None
        TBL = 1.28
        while done < n:
            best = None
            for e in ENGS:
                r = ready[e]
                if not r:
                    continue
                f = free[e]
                for i in r[:WIN]:
                    t = drt[i] if drt[i] > f else f
                    if e == "act" and ops[i][6] is not None and ops[i][6] != cur_tbl:
                        t += TBL
                    if best is None or t < best[0] or (t == best[0] and i < best[2]):
                        best = (t, e, i)
            start, e, i = best
            ready[e].remove(i)
            o = ops[i]
            if e == "act" and o[6] is not None:
                cur_tbl = o[6]
            if o[3] is not None:
                free[e] = start + (6.0 if e == "pool" else 0.1)
                finish[i] = start + o[4]
            else:
                free[e] = start + o[4]
                finish[i] = free[e]
            order[e].append(i)
            done += 1
            for s_ in succ[i]:
                if i in ops[s_][5]:
                    t = start
                else:
                    t = finish[i] + (0.0 if (ops[s_][0] == e and o[3] is None) else self.LAT)
                if t > drt[s_]:
                    drt[s_] = t
                indeg[s_] -= 1
                if indeg[s_] == 0:
                    r = ready[ops[s_][0]]
                    lo, hi = 0, len(r)
                    while lo < hi:
                        mid = (lo + hi) // 2
                        if r[mid] < s_:
                            lo = mid + 1
                        else:
                            hi = mid
                    r.insert(lo, s_)
        self.makespan = max(finish) if n else 0.0
        return order

    def emit(self, final_waits=(), do_schedule=True):
        nc = self.nc
        ops = self.ops
        if do_schedule:
            order = self.schedule()
        else:
            order = {e: [i for i, o in enumerate(ops) if o[0] == e] for e in ENGS}
        wdeps = []
        needed = {e: set() for e in ENGS}
        for i, o in enumerate(ops):
            e = o[0]
            wl = []
            for d, is_raw in o[2].items():
                od = ops[d]
                if od[3] is None and od[0] == e:
                    if e in ("pe", "sp") or not (is_raw or o[3] is not None):
                        continue
                wl.append(d)
                if od[3] is None:
                    needed[od[0]].add(d)
            wdeps.append(wl)
        semv = {}
        for e in ENGS:
            c = 0
            for i in order[e]:
                if i in needed[e]:
                    c += 1
                    semv[i] = c
        dcount = {}
        dval = {}
        for e in ENGS:
            for i in order[e]:
                nm = ops[i][3]
                if nm is not None:
                    dcount[nm] = dcount.get(nm, 0) + 16
                    dval[i] = dcount[nm]
        with ExitStack() as st:
            esem = {e: st.enter_context(nc.semaphore("s_" + e)) for e in ENGS}
            dsem = {nm: st.enter_context(nc.semaphore("d_" + nm)) for nm in dcount}
            block = st.enter_context(nc.Block())

            def run(engname, e):
                seen = {}
                for i in order[engname]:
                    o = ops[i]
                    waits = {}
                    for d in wdeps[i]:
                        od = ops[d]
                        if od[3] is not None:
                            key, val = ("dma", od[3]), dval[d]
                        else:
                            key, val = ("eng", od[0]), semv[d]
                        if seen.get(key, -1) >= val:
                            continue
                        if waits.get(key, -1) < val:
                            waits[key] = val
                    for key, val in waits.items():
                        seen[key] = val
                        e.wait_ge(esem[key[1]] if key[0] == "eng" else dsem[key[1]], val)
                    ins = o[1](e)
                    if o[3] is not None:
                        ins.then_inc(dsem[o[3]], 16)
                    elif i in semv:
                        ins.then_inc(esem[engname], 1)
                if engname == "sp":
                    for i in final_waits:
                        e.wait_ge(dsem[ops[i][3]], dcount[ops[i][3]])

            @block.tensor
            def _(e):
                run("pe", e)

            @block.scalar
            def _(e):
                run("act", e)

            @block.vector
            def _(e):
                run("dve", e)

            @block.gpsimd
            def _(e):
                run("pool", e)

            @block.sync
            def _(e):
                run("sp", e)


def hsb_list():
    L = []
    for c0 in range(0, S_DT, HS):
        L.append(("w_in", 0, c0))
    for c0 in range(G_A, DIN, HS):
        L.append(("w_in", 0, c0))
    for c0 in range(0, D, HS):
        L.append(("w_a_out", 0, c0))
    for c0 in range(0, D, HS):
        L.append(("w_o", 0, c0))
    for c0 in range(0, D, HS):
        for kh in range(2):
            L.append(("w_b_out", kh * 2048, c0))
    return L


HSB = hsb_list()
NH = len(HSB)


def h_in(col):
    if col < S_DT:
        return col // HS, (col % HS) // 128
    c = col - G_A
    return S_DT // HS + c // HS, (c % HS) // 128


H_WA = S_DT // HS + (DIN - G_A) // HS
H_WO = H_WA + D // HS
H_WB = H_WO + D // HS


class _Stop(Exception):
    pass


def build_nc(dbg=None):
    nc = bass.Bass("TRN2", target_bir_lowering=False)
    dram_in = lambda n, s: nc.dram_tensor(n, s, F32, kind="ExternalInput").ap()
    x_main = dram_in("x_main", [NTOK, D])
    x_prev = dram_in("x_prev", [NTOK, D])
    W = {"w_in": dram_in("w_in", [D, DIN]), "w_a_out": dram_in("w_a_out", [D, D]),
         "w_b_out": dram_in("w_b_out", [2 * D, D]), "w_o": dram_in("w_o", [D, D])}
    pcol_d = dram_in("pcol", [128, NPC])
    prow_d = dram_in("prow", [128, 192])
    finw_d = dram_in("finw", [128, D])
    out_d = nc.dram_tensor("out", [NTOK, D], F32, kind="ExternalOutput").ap()
    wsc = nc.dram_tensor("wsc", [NH, 128, KC, HS], BF16).ap()
    dbg_d = nc.dram_tensor("dbg", [128, 16384], F32, kind="ExternalOutput").ap() if dbg else None

    P = Prog(nc)
    with ExitStack() as st:
        sbt = lambda n, s, d: st.enter_context(nc.sbuf_tensor("sb_" + n, s, d))
        pst = lambda n, s, d: st.enter_context(nc.psum_tensor(n, s, d))
        pb = [pst("pb%d" % i, [128, 512], F32) for i in range(7)]
        pb7 = pst("pb7", [128, 1024], BF16)
        hT = sbt("hT", [128, KC, T], BF16)
        NSLOT = 3
        wslot = [sbt("wslot%d" % i, [128, KC, HS], BF16) for i in range(NSLOT)]
        R1 = sbt("R1", [128, 16 * T], F32)
        va = sbt("va", [128, KC, T], BF16)
        ma = sbt("ma", [128, KC, T], BF16)
        BT2 = [sbt("BT%d" % i, [128, 2, T], BF16) for i in range(2)]
        CT2 = [sbt("CT%d" % i, [128, 2, T], BF16) for i in range(2)]
        xTg2 = [sbt("xTg%d" % i, [128, 4, T], BF16) for i in range(2)]
        zsgbuf = sbt("zsgbuf", [128, 2 * 4 * T], BF16)
        zsg2 = [zsgbuf[:, i * 4 * T:(i + 1) * 4 * T].rearrange("p (b t) -> p b t", b=4) for i in range(2)]
        te2 = [sbt("te%d" % i, [128, 256], F32) for i in range(2)]
        dec2 = [sbt("dec%d" % i, [128, 256], F32) for i in range(2)]
        stT = sbt("stT", [128, 8, 512], F32)
        stb = sbt("stb", [128, 512], BF16)
        R3 = sbt("R3", [128, 2048], F32)
        R2 = sbt("R2", [128, 2048], F32)
        upad = sbt("upad", [128, 30 + T], BF16)
        ucarry = sbt("ucarry", [128, 16, 30], BF16)
        sgt = sbt("sgt", [128, T], F32)
        sqb2 = [sbt("sqb%d" % i, [128, T], BF16) for i in range(2)]
        ucb2 = [sbt("ucb%d" % i, [128, T], BF16) for i in range(2)]
        rstd = sbt("rstd", [128, T], F32)
        saz = sbt("saz", [128, T], F32)
        xrawb = [sbt("xrawb%d" % i, [128, 4 + T], BF16) for i in range(2)]
        diag4 = [sbt("diag4_%d" % i, [128, 4, 128], BF16) for i in range(2)]
        cacc = sbt("cacc", [128, T], F32)
        carryb = sbt("carryb", [128, 48, 3], BF16)
        Zhi = sbt("Zhi", [128, 8, 128], BF16)
        Zlo = sbt("Zlo", [128, 8, 128], BF16)
        cbTm4 = sbt("cbTm4", [128, 4, 128], BF16)
        xp2 = [sbt("xp%d" % i, [128, 512], BF16) for i in range(2)]
        xpp2 = [sbt("xpp%d" % i, [128, 512], BF16) for i in range(2)]
        btm2 = [sbt("btm%d" % i, [128, 128], BF16) for i in range(2)]
        dtt = sbt("dtt", [128, 256], F32)
        da = sbt("da", [128, 256], F32)
        cum = sbt("cum", [128, 256], F32)
        ecum = sbt("ecum", [128, 256], F32)
        dahi = sbt("dahi", [128, 256], BF16)
        dalo = sbt("dalo", [128, 256], BF16)
        ssc = sbt("ssc", [128, 2], F32)
        ssc2 = sbt("ssc2", [128, 2], F32)
        ssc3 = sbt("ssc3", [128, 2], F32)
        ones_b = sbt("ones_b", [128, 128], BF16)
        dummy = sbt("dmy_t", [128, 2], F32)
        pcol = sbt("pcol", [128, NPC], F32)
        prow = sbt("prow", [128, 192], F32)
        Arow = sbt("Arow", [128, 64], F32)
        wdt = sbt("wdt", [128, KC, 64], BF16)
        ident_f = sbt("ident_f", [128, 128], F32)
        ones_f = sbt("ones_f", [128, 128], F32)
        U_f = sbt("U_f", [128, 128], F32)
        L_f = sbt("L_f", [128, 128], F32)
        ident_b = sbt("ident_b", [128, 128], BF16)
        U_b = sbt("U_b", [128, 128], BF16)
        L_b = sbt("L_b", [128, 128], BF16)

        msq = sgt
        dtx, dta, dtl, dahf = sgt[:, 0:256], sgt[:, 256:512], saz[:, 0:256], saz[:, 256:512]
        R2b = R2[:].bitcast(BF16)
        M1q = [R2b[:, 0:1024].rearrange("p (r l) -> p r l", r=8), R2b[:, 1024:2048].rearrange("p (r l) -> p r l", r=8)]
        ysb = R2[:, 1024:1536]
        xDq = [R2b[:, 3072:3584], R2b[:, 3584:4096]]
        mu = cacc
        t1 = cacc
        uconv = R1[:].rearrange("p (j t) -> p j t", j=16)
        ynT = R1[:].bitcast(BF16).rearrange("p (j t) -> p j t", j=32)
        xres = R1[:].rearrange("p (b d) -> p b d", b=4)
        xin = R3
        yg = R3[:].rearrange("p (b t) -> p b t", b=4)
        xinB = zsgbuf[:].bitcast(F32)
        ma_f = ma[:].rearrange("p a b -> p (a b)").bitcast(F32)
        finw_v, sq_v = ma_f[:, 0:2048], ma_f[:, 2048:4096]
        sqtmp = R2
        diag = R2[:].bitcast(BF16)[:, 0:31 * 128].rearrange("p (k c) -> p k c", k=31)

        C = ["pcol", "prow", "ident_f", "ones_f", "U_f", "L_f", "ident_b", "U_b", "L_b", "Arow", "wdt", "ones_b"]

        def handover(keys):
            P.op("dve", lambda e: e.tensor_copy(out=dummy[:, 0:1], in_=dummy[:, 1:2]), [], keys)

        def fsz(ap):
            n_ = 1
            for d_ in ap.shape[1:]:
                n_ *= d_
            return n_

        def mm(out, lhsT, rhs, start, stop, r, w, cr=()):
            c_ = max(0.07, fsz(rhs) / 2200.0 * (4 if rhs.dtype == F32 else 1))
            P.op("pe", lambda e: e.matmul(out, lhsT=lhsT, rhs=rhs, start=start, stop=stop), r, w, creads=cr, cost=c_)

        def tr(out, in_, ident, r, w):
            P.op("pe", lambda e: e.transpose(out=out, in_=in_, identity=ident), r, w, creads=C,
                 cost=0.25 if in_.dtype == F32 else 0.08)

        def act(out, in_, func, r, w, scale=None, bias=None, eng="act"):
            kw = {}
            if scale is not None:
                kw["scale"] = scale
            if bias is not None:
                kw["bias"] = bias
            tbl = {AF.Exp: "exp", AF.Silu: "silu", AF.Sigmoid: "sig", AF.Sqrt: "sqrt", AF.Ln: "ln"}.get(func)
            P.op(eng, lambda e: e.activation(out=out, in_=in_, func=func, **kw), r, w, creads=C,
                 cost=0.28 + fsz(out) / 1150.0, tbl=tbl)

        def vcost(eng, out):
            return (0.15 + fsz(out) / 900.0) if eng == "dve" else (0.3 + fsz(out) / 450.0)

        def tt(eng, out, in0, in1, op, r, w):
            P.op(eng, lambda e: e.tensor_tensor(out=out, in0=in0, in1=in1, op=op), r, w, creads=C, cost=vcost(eng, out))

        def ts(eng, out, in0, s1, s2, op0, op1, r, w):
            if op1 is None:
                P.op(eng, lambda e: e.tensor_scalar(out=out, in0=in0, scalar1=s1, scalar2=None, op0=op0), r, w, creads=C,
                     cost=vcost(eng, out))
            else:
                P.op(eng, lambda e: e.tensor_scalar(out=out, in0=in0, scalar1=s1, scalar2=s2, op0=op0, op1=op1), r, w, creads=C,
                     cost=vcost(eng, out))

        def stt(eng, out, in0, scalar, in1, op0, op1, r, w):
            P.op(eng, lambda e: e.scalar_tensor_tensor(out=out, in0=in0, scalar=scalar, in1=in1, op0=op0, op1=op1), r, w,
                 creads=C, cost=vcost(eng, out))

        def cp(eng, out, in_, r, w):
            P.op(eng, lambda e: e.tensor_copy(out=out, in_=in_), r, w, creads=C, cost=vcost(eng, out))

        def recip(ap_, r, w):
            P.op("dve", lambda e: e.reciprocal(out=ap_, in_=ap_), r, w, cost=0.15 + fsz(ap_) * 6.5 / 960.0)

        def dma(eng, out, in_, r, w, sem):
            nbytes = fsz(out) * out.shape[0] * (2 if out.dtype == BF16 else 4)
            return P.op(eng, lambda e: e.dma_start(out=out, in_=in_), r, w, dma=sem, cost=2.0 + nbytes / 200e3)

        def pc(i):
            return pcol[:, i:i + 1]

        dma("sp", pcol[:], pcol_d, [], ["pcol"], "c0")
        dma("sp", prow[:], prow_d, [], ["prow"], "c1")
        P.op("dve", lambda e: e.memset(ident_f[:], 0.0), [], ["ident_f"])
        P.op("dve", lambda e: e.memset(ones_f[:], 1.0), [], ["ones_f"])
        P.op("pool", lambda e: e.affine_select(out=ident_f[:], in_=ident_f[:], compare_op=ALU.not_equal, fill=1.0,
                                               base=0, pattern=[[-1, 128]], channel_multiplier=1), ["ident_f"], ["ident_f"])
        P.op("pool", lambda e: e.affine_select(out=U_f[:], in_=ones_f[:], compare_op=ALU.is_ge, fill=0.0,
                                               base=0, pattern=[[1, 128]], channel_multiplier=-1), ["ones_f"], ["U_f"])
        P.op("pool", lambda e: e.affine_select(out=L_f[:], in_=ones_f[:], compare_op=ALU.is_ge, fill=0.0,
                                               base=-1, pattern=[[-1, 128]], channel_multiplier=1), ["ones_f"], ["L_f"])
        cp("dve", ident_b[:], ident_f[:], ["ident_f"], ["ident_b"])
        cp("dve", U_b[:], U_f[:], ["U_f"], ["U_b"])
        cp("dve", L_b[:], L_f[:], ["L_f"], ["L_b"])
        cp("dve", ones_b[:], ones_f[:], ["ones_f"], ["ones_b"])
        P.op("dve", lambda e: e.memset(stT[:], 0.0), [], ["stT"])
        P.op("dve", lambda e: e.memset(ucarry[:], 0.0), [], [("ucarry", j) for j in range(16)])
        P.op("dve", lambda e: e.memset(carryb[:], 0.0), [], [("carryb", j) for j in range(48)])
        act(Arow[:], prow[:, 64:128], AF.Exp, ["prow"], ["Arow"])
        ts("dve", Arow[:], Arow[:], -1.0, None, ALU.mult, None, ["Arow"], ["Arow"])
        P.op("dve", lambda e: e.memset(dummy[:], 0.0), [], [])
        dma("pool", wdt[:], W["w_in"][:, S_DT:S_DT + 64].rearrange("(kc p) c -> p kc c", p=128), [], ["wdt"], "cwdt")

        def cast_h(h):
            name, k0, c0 = HSB[h]
            src = W[name][k0:k0 + 2048, c0:c0 + HS].rearrange("(kc p) c -> p kc c", p=128)
            dma("pool", wsc[h], src, [], [("wsc", h)], "cast%d" % (h % 56))

        order = []
        xh = [h_in(S_X + i * HS)[0] for i in range(16)]
        bh = [h_in(S_B + i * HS)[0] for i in range(4)]
        for g in range(8):
            if g % 2 == 0:
                order.append(bh[g // 2])
            order += xh[2 * g:2 * g + 2]
        order += [h_in(A_GATE + i * HS)[0] for i in range(8)]
        order += [h_in(A_VAL + i * HS)[0] for i in range(8)]
        for h in range(NH):
            if h not in order:
                order.append(h)
        if dbg and "ncast" in dbg:
            order = order[:dbg["ncast"]]
        for h in order:
            cast_h(h)

        wctr = [0]
        xbc_ctr = [0]

        def getw(h):
            s = wctr[0] % NSLOT
            wctr[0] += 1
            dma("sp", wslot[s][:], wsc[h], [("wsc", h)], [("ws", s)], "wl%d" % s)
            return wslot[s], ("ws", s)

        bank_rr = [0]

        def inproj(wt, wk, cb, bank, t_lo=0):
            for kc in range(KC):
                mm(pb[bank][:, t_lo:T], wt[:, kc, cb * 128:(cb + 1) * 128], hT[:, kc, t_lo:T], kc == 0, kc == KC - 1,
                   [wk, "hT"], ["pb%d" % bank])

        banks = [[0, 1]]
        tctr = [0]

        def setbanks(l):
            banks[0] = [0, 1]

        def nextbank():
            b = banks[0][bank_rr[0] % len(banks[0])]
            bank_rr[0] += 1
            return b

        def stage_end(name, mode):
            if dbg and dbg.get("stop") == name and dbg.get("mode") == mode:
                raise _Stop()

        def tile(x_src, t0, mode, out_ap=None):
            main = mode == "main"
            tctr[0] += 1
            te, dec = te2[tctr[0] % 2], dec2[tctr[0] % 2]
            kte, kdec = ("te", tctr[0] % 2), ("dec", tctr[0] % 2)
            setbanks([0, 1])
            handover([("zsg", 0), ("zsg", 1), "xinB"])
            for tb in range(4):
                xin_, kxin = (R3[:], "R3") if tb % 2 == 0 else (xinB, "xinB")
                sc_, ksc = (ssc, "ssc") if tb % 2 == 0 else (ssc3, "ssc3")
                dma("sp", xin_, x_src[t0 + tb * 128:t0 + (tb + 1) * 128, :], [], [kxin], "xin%d" % (tb % 2))
                act(sqtmp[:], xin_, AF.Square, [kxin], ["R2"])
                P.op("dve", lambda e, sc_=sc_: e.reduce_sum(out=sc_[:, 0:1], in_=sqtmp[:], axis=AX.X), ["R2"], [ksc], cost=2.4)
                ts("dve", sc_[:, 1:2], sc_[:, 0:1], 1.0 / D, NORM_EPS, ALU.mult, ALU.add, [ksc], [ksc])
                act(sc_[:, 1:2], sc_[:, 1:2], AF.Sqrt, [ksc], [ksc])
                recip(sc_[:, 1:2], [ksc], [ksc])
                ts("dve", xin_, xin_, sc_[:, 1:2], None, ALU.mult, None, [kxin, ksc], [kxin])
                for kc in range(KC):
                    q = kc // 4
                    tr(pb[2 + q][:, (kc % 4) * 128:(kc % 4 + 1) * 128], xin_[:, kc * 128:(kc + 1) * 128], ident_f[:],
                       [kxin], ["pb%d" % (2 + q)])
                for kc in range(KC):
                    q = kc // 4
                    act(hT[:, kc, tb * 128:(tb + 1) * 128], pb[2 + q][:, (kc % 4) * 128:(kc % 4 + 1) * 128], AF.Copy,
                        ["pb%d" % (2 + q)], ["hT"], scale=pc(PC_NORMW + kc))

            handover([("zsg", 0), ("zsg", 1), "xinB"])
            stage_end("H", mode)
            for c in range(4):
                for kc in range(KC):
                    mm(pb[2][:, c * 64:(c + 1) * 64], hT[:, kc, c * 128:(c + 1) * 128], wdt[:, kc, :], kc == 0, kc == KC - 1,
                       ["hT"], ["pb2"], cr=["wdt"])
            v3 = lambda a: a.rearrange("p (c r) -> p c r", c=4)
            tt("dve", v3(dtx[:]), v3(pb[2][:, 0:256]), prow[:, 0:64].unsqueeze(1).to_broadcast([128, 4, 64]), ALU.add,
               ["pb2"], ["sgt"])
            ts("dve", dtl[:], dtx[:], 0.0, None, ALU.max, None, ["sgt"], ["saz"])
            stt("dve", dta[:], dtl[:], -2.0, dtx[:], ALU.mult, ALU.add, ["sgt", "saz"], ["sgt"])
            act(dta[:], dta[:], AF.Exp, ["sgt"], ["sgt"])
            ts("dve", dta[:], dta[:], 1.0, None, ALU.add, None, ["sgt"], ["sgt"])
            act(dta[:], dta[:], AF.Ln, ["sgt"], ["sgt"])
            tt("dve", dtt[:], dtl[:], dta[:], ALU.add, ["sgt", "saz"], ["dtt"])
            tt("dve", v3(da[:]), v3(dtt[:]), Arow[:].unsqueeze(1).to_broadcast([128, 4, 64]), ALU.mult, ["dtt"], ["da"])
            mm(pb[3][:, 0:256], U_f[:], da[:], True, True, ["da"], ["pb3"], cr=C)
            mm(pb[4][:, 0:256], ones_f[:], da[:], True, True, ["da"], ["pb4"], cr=C)
            cp("dve", cum[:], pb[3][:, 0:256], ["pb3"], ["cum"])
            act(ecum[:], cum[:], AF.Exp, ["cum"], ["ecum"])
            tt("dve", te[:], pb[4][:, 0:256], cum[:], ALU.subtract, ["pb4", "cum"], [kte])
            act(te[:], te[:], AF.Exp, [kte], [kte])
            tt("dve", te[:], te[:], dtt[:], ALU.mult, [kte, "dtt"], [kte])
            act(dec[:], pb[4][:, 0:256], AF.Exp, ["pb4"], [kdec])
            cp("dve", dahi[:], da[:], ["da"], ["dahi"])
            cp("dve", dahf[:], dahi[:], ["dahi"], ["saz"])
            tt("dve", dalo[:], da[:], dahf[:], ALU.subtract, ["da", "saz"], ["dalo"])

            stage_end("DT", mode)
            setbanks([0, 1, 3, 4] if main else [0, 1])
            def a1_inproj(j):
                if j % 2 == 0:
                    a1_inproj.wg = getw(h_in(A_GATE + j * 128)[0])
                    a1_inproj.wv = getw(h_in(A_VAL + j * 128)[0])
                cb = j % 2
                lo = 0 if main else T - 128
                b0 = nextbank()
                inproj(a1_inproj.wg[0], a1_inproj.wg[1], cb, b0, lo)
                act(sgt[:, lo:T], pb[b0][:, lo:T], AF.Sigmoid, ["pb%d" % b0], ["sgt"])
                b1 = nextbank()
                inproj(a1_inproj.wv[0], a1_inproj.wv[1], cb, b1, lo)
                if main:
                    cp("dve", upad[:, 0:30], ucarry[:, j, :], [("ucarry", j)], ["upad"])
                tt("dve", upad[:, 30 + lo:30 + T], pb[b1][:, lo:T], sgt[:, lo:T], ALU.mult, ["pb%d" % b1, "sgt"], ["upad"])
                cp("dve", ucarry[:, j, :], upad[:, T:T + 30], ["upad"], [("ucarry", j)])

            def a1_conv(j):
                P.op("dve", lambda e: e.tensor_tensor(
                    out=diag, in0=ident_b[:].unsqueeze(1).to_broadcast([128, 31, 128]),
                    in1=pcol[:, PC_CAW + j * 31:PC_CAW + (j + 1) * 31].unsqueeze(2).to_broadcast([128, 31, 128]),
                    op=ALU.mult), [], ["R2"], creads=C, cost=4.4)
                for k in range(31):
                    mm(pb[2][:], diag[:, k, :], upad[:, k:k + T], k == 0, k == 30, ["R2", "upad"], ["pb2"])
                act(uconv[:, j, :], pb[2][:], AF.Identity, ["pb2"], [("R1", j)], bias=pc(PC_CAB + j))
                act(ucb2[j % 2][:], pb[2][:], AF.Identity, ["pb2"], [("ucb", j % 2)], bias=pc(PC_CAB + j))
                act(sqb2[j % 2][:], uconv[:, j, :], AF.Square, [("R1", j)], [("sqb", j % 2)])
                mm(pb[5][:], ones_b[:], ucb2[j % 2][:], j == 0, j == 15, [("ucb", j % 2)], ["pb5"], cr=C)
                mm(pb[6][:], ones_b[:], sqb2[j % 2][:], j == 0, j == 15, [("sqb", j % 2)], ["pb6"], cr=C)

            if main or mode == "pre_last":
                for j in range(16):
                    a1_inproj(j)
                    if main:
                        a1_conv(j)
            stage_end("A1", mode)
            if main:
                ts("dve", mu[:], pb[5][:], 1.0 / D, None, ALU.mult, None, ["pb5"], ["cacc"])
                tt("dve", msq[:], mu[:], mu[:], ALU.mult, ["cacc"], ["sgt"])
                stt("dve", rstd[:], pb[6][:], 1.0 / D, msq[:], ALU.mult, ALU.subtract, ["pb6", "sgt"], ["rstd"])
                ts("dve", rstd[:], rstd[:], LN_EPS, None, ALU.add, None, ["rstd"], ["rstd"])
                act(rstd[:], rstd[:], AF.Sqrt, ["rstd"], ["rstd"])
                recip(rstd[:], ["rstd"], ["rstd"])
                for j in range(16):
                    if j % 2 == 0:
                        wz = getw(h_in(A_Z + j * 128)[0])
                    b0 = nextbank()
                    inproj(wz[0], wz[1], j % 2, b0)
                    act(saz[:], pb[b0][:], AF.Silu, ["pb%d" % b0], ["saz"])
                    tt("dve", uconv[:, j, :], uconv[:, j, :], mu[:], ALU.subtract, [("R1", j), "cacc"], [("R1", j)])
                    tt("dve", uconv[:, j, :], uconv[:, j, :], rstd[:], ALU.mult, [("R1", j), "rstd"], [("R1", j)])
                    act(uconv[:, j, :], uconv[:, j, :], AF.Silu, [("R1", j)], [("R1", j)], scale=pc(PC_LNW + j), bias=pc(PC_LNB + j))
                    tt("dve", va[:, j, :], uconv[:, j, :], saz[:], ALU.mult, [("R1", j), "saz"], [("va", j)])
                stage_end("A2", mode)
                for jo in range(16):
                    if jo % 2 == 0:
                        wa = getw(H_WA + jo // 2)
                        wg = getw(h_in(G_A + jo * 128)[0])
                    cb = jo % 2
                    b0 = nextbank()
                    inproj(wg[0], wg[1], cb, b0)
                    act(sgt[:], pb[b0][:], AF.Sigmoid, ["pb%d" % b0], ["sgt"])
                    b1 = nextbank()
                    for kc in range(KC):
                        mm(pb[b1][:], wa[0][:, kc, cb * 128:(cb + 1) * 128], va[:, kc, :], kc == 0, kc == KC - 1,
                           [wa[1]] + [("va", kc)], ["pb%d" % b1])
                    tt("dve", ma[:, jo, :], pb[b1][:], sgt[:], ALU.mult, ["pb%d" % b1, "sgt"], [("ma", jo)])

            stage_end("A3", mode)
            setbanks([0, 1] if main else [0, 1, 2, 3, 5, 6])
            def xbc_block(wt, wk, cb, blk, dst, dkey):
                q = xbc_ctr[0] % 2
                xbc_ctr[0] += 1
                b0 = nextbank()
                inproj(wt, wk, cb, b0)
                cp("dve", xrawb[q][:, 0:3], carryb[:, blk, :], [("carryb", blk)], [("xraw", q)])
                act(xrawb[q][:, 3:3 + T], pb[b0][:], AF.Copy, ["pb%d" % b0], [("xraw", q)])
                cp("dve", carryb[:, blk, :], xrawb[q][:, T:T + 3], [("xraw", q)], [("carryb", blk)])
                P.op("dve", lambda e: e.tensor_tensor(
                    out=diag4[q][:], in0=ident_b[:].unsqueeze(1).to_broadcast([128, 4, 128]),
                    in1=pcol[:, PC_CBW + blk * 4:PC_CBW + blk * 4 + 4].unsqueeze(2).to_broadcast([128, 4, 128]),
                    op=ALU.mult), [], [("diag4", q)], creads=C, cost=0.7)
                b1 = b0
                for k in range(4):
                    mm(pb[b1][:], diag4[q][:, k, :], xrawb[q][:, k:k + T], k == 0, k == 3, [("diag4", q), ("xraw", q)], ["pb%d" % b1])
                act(dst, pb[b1][:], AF.Silu, ["pb%d" % b1], [dkey], bias=pc(PC_CBB + blk))

            pre_last = mode == "pre_last"

            def prep_list(g):
                L = []
                par, pp = g % 2, (g // 2) % 2
                hold = {}
                if par == 0:
                    def fB(cb):
                        if cb == 0:
                            hold["B"] = getw(h_in(S_B + g * 128)[0])
                        xbc_block(hold["B"][0], hold["B"][1], cb, 32 + g + cb, BT2[pp][:, cb, :], ("BT", pp))
                    L += [lambda cb=cb: fB(cb) for cb in range(2)]
                    if main or pre_last:
                        def fC(cb):
                            if cb == 0:
                                hold["C"] = getw(h_in(S_C + g * 128)[0])
                            if main:
                                xbc_block(hold["C"][0], hold["C"][1], cb, 40 + g + cb, CT2[pp][:, cb, :], ("CT", pp))
                            else:
                                b0 = nextbank()
                                inproj(hold["C"][0], hold["C"][1], cb, b0, T - 128)
                                act(carryb[:, 40 + g + cb, :], pb[b0][:, T - 3:T], AF.Copy, ["pb%d" % b0], [("carryb", 40 + g + cb)])
                        L += [lambda cb=cb: fC(cb) for cb in range(2)]

                def fx(b):
                    if b % 2 == 0:
                        hold["x"] = getw(h_in(S_X + (4 * g + b) * 128)[0])
                    xbc_block(hold["x"][0], hold["x"][1], b % 2, 4 * g + b, xTg2[par][:, b, :], ("xTg", par))
                L += [lambda b=b: fx(b) for b in range(4)]
                if main:
                    def fz(b):
                        if b % 2 == 0:
                            hold["z"] = getw(h_in(S_Z + (4 * g + b) * 128)[0])
                        b0 = nextbank()
                        inproj(hold["z"][0], hold["z"][1], b % 2, b0)
                        act(zsg2[par][:, b, :], pb[b0][:], AF.Silu, ["pb%d" % b0], [("zsg", par)])
                    L += [lambda b=b: fz(b) for b in range(4)]
                return L

            for f in prep_list(0):
                f()
            if main:
                handover(["R2", "M1a", "M1b", "ysb", "xDa", "xDb"])
            h3 = lambda a: a.rearrange("p (r q) -> p r q", r=8)
            bc_p = lambda a: a.unsqueeze(2).to_broadcast([128, 8, 64])
            M1k, xDk = ["M1a", "M1b"], ["xDa", "xDb"]

            def grp(g):
                par, pp = g % 2, (g // 2) % 2
                return dict(gl=g % 2, xTg=xTg2[par], zsg=zsg2[par], BT=BT2[pp], CT=CT2[pp],
                            kx=("xTg", par), kz=("zsg", par), kB=("BT", pp), kC=("CT", pp))

            def cbT4(g):
                G = grp(g)
                for c in range(4):
                    cs = slice(c * 128, (c + 1) * 128)
                    mm(pb[4][:, cs], G["BT"][:, G["gl"], cs], G["CT"][:, G["gl"], cs], True, True, [G["kB"], G["kC"]], ["pb4"])
                tt("dve", cbTm4[:], pb[4][:].rearrange("p (c l) -> p c l", c=4), U_f[:].unsqueeze(1).to_broadcast([128, 4, 128]),
                   ALU.mult, ["pb4"], ["cbTm4"])

            def S1(g, c, q):
                G = grp(g)
                cs = slice(c * 128, (c + 1) * 128)
                rs = slice(c * 64 + g * 8, c * 64 + g * 8 + 8)
                for b in range(4):
                    tr(pb7[:, b * 128:(b + 1) * 128], G["xTg"][:, b, cs], ident_b[:], [G["kx"]], ["pb7"])
                tr(pb7[:, 512:640], G["BT"][:, G["gl"], cs], ident_b[:], [G["kB"]], ["pb7"])
                cp("dve", btm2[q][:], pb7[:, 512:640], ["pb7"], [("btm", q)])
                tt("dve", h3(xpp2[q][:]), h3(pb7[:, 0:512]), bc_p(te[:, rs]), ALU.mult, ["pb7", kte], [("xpp", q)])
                if not main:
                    return
                tt("dve", h3(xp2[q][:]), h3(pb7[:, 0:512]), bc_p(dtt[:, rs]), ALU.mult, ["pb7", "dtt"], [("xp", q)])
                tt("dve", h3(xDq[q]), h3(pb7[:, 0:512]), bc_p(prow[:, 128 + g * 8:128 + g * 8 + 8]), ALU.mult,
                   ["pb7"], [xDk[q]])
                ub = U_b[:].unsqueeze(1).to_broadcast([128, 8, 128])
                tt("pool", Zhi[:], ub, dahi[:, rs].unsqueeze(2).to_broadcast([128, 8, 128]), ALU.mult, ["dahi"], ["Zhi"])
                for hh in range(2):
                    zs_ = slice(hh * 4, hh * 4 + 4)
                    mm(pb[5 + hh][:], L_b[:], Zhi[:, zs_, :].rearrange("p r l -> p (r l)"), True, True, ["Zhi"], ["pb%d" % (5 + hh)], cr=C)
                    act(M1q[q][:, zs_, :].rearrange("p r l -> p (r l)"), pb[5 + hh][:], AF.Exp, ["pb%d" % (5 + hh)], [M1k[q]])
                tt("dve", M1q[q], M1q[q], cbTm4[:, c, :].unsqueeze(1).to_broadcast([128, 8, 128]), ALU.mult,
                   [M1k[q], "cbTm4"], [M1k[q]])

            seq = [(g, c) for g in range(8) for c in range(4)]
            nxt = prep_list(1)

            def fill(n=1):
                for _ in range(n):
                    if nxt:
                        nxt.pop(0)()

            if main:
                cbT4(0)
            S1(0, 0, 0)
            for n, (g, c) in enumerate(seq):
                q = n % 2
                G = grp(g)
                cs = slice(c * 128, (c + 1) * 128)
                rs = slice(c * 64 + g * 8, c * 64 + g * 8 + 8)
                if main:
                    mm(pb[2][:], ident_b[:], xDq[q], True, False, [xDk[q]], ["pb2"], cr=C)
                    for r in range(8):
                        mm(pb[2][:, r * 64:(r + 1) * 64], M1q[q][:, r, :], xp2[q][:, r * 64:(r + 1) * 64], False, r == 7,
                           [M1k[q], ("xp", q)], ["pb2"])
                    act(stb[:], stT[:, g, :], AF.Copy, [("stT", g)], ["stb"])
                    mm(pb[3][:], G["CT"][:, G["gl"], cs], stb[:], True, True, [G["kC"], "stb"], ["pb3"])
                mm(pb[4][:], btm2[q][:], xpp2[q][:], True, True, [("btm", q), ("xpp", q)], ["pb4"])
                if main:
                    tt("dve", h3(ysb), h3(pb[3][:]), bc_p(ecum[:, rs]), ALU.mult, ["pb3", "ecum"], ["ysb"])
                    tt("dve", ysb, ysb, pb[2][:], ALU.add, ["ysb", "pb2"], ["ysb"])
                tt("dve", h3(stT[:, g, :]), h3(stT[:, g, :]), bc_p(dec[:, rs]), ALU.mult, [("stT", g), kdec], [("stT", g)])
                tt("dve", stT[:, g, :], stT[:, g, :], pb[4][:], ALU.add, [("stT", g), "pb4"], [("stT", g)])
                fill(1 if main else (2 if c % 2 == 0 else 1))
                if main:
                    for b in range(4):
                        tr(pb[3][:, b * 128:(b + 1) * 128], ysb[:, b * 128:(b + 1) * 128], ident_f[:], ["ysb"], ["pb3"])
                    tt("dve", yg[:, :, cs], pb[3][:].rearrange("p (b l) -> p b l", b=4), G["zsg"][:, :, cs], ALU.mult,
                       ["pb3", G["kz"]], ["R3"])
                if c == 3:
                    while nxt:
                        fill(1)
                    if main:
                        for b in range(4):
                            act(sqb2[b % 2][:], yg[:, b, :], AF.Square, ["R3"], [("sqb", b % 2)])
                            mm(pb[4][:], ones_b[:], sqb2[b % 2][:], b == 0, b == 3, [("sqb", b % 2)], ["pb4"], cr=C)
                        ts("dve", rstd[:], pb[4][:], 1.0 / 512, NORM_EPS, ALU.mult, ALU.add, ["pb4"], ["rstd"])
                        act(rstd[:], rstd[:], AF.Sqrt, ["rstd"], ["rstd"])
                        recip(rstd[:], ["rstd"], ["rstd"])
                        for b in range(4):
                            if g == 0 and b == 0:
                                handover([("R1", j) for j in range(16)] + [("yn", k) for k in range(32)])
                            stt("dve", ynT[:, 4 * g + b, :], yg[:, b, :], pc(PC_SNW + 4 * g + b), rstd[:], ALU.mult, ALU.mult,
                                ["R3", "rstd"], [("yn", 4 * g + b)])
                    if g < 7:
                        nxt.extend(prep_list(g + 2) if g + 2 < 8 else [])
                        if main:
                            cbT4(g + 1)
                if n + 1 < len(seq):
                    S1(seq[n + 1][0], seq[n + 1][1], (n + 1) % 2)
                fill(1 if main else 0)
            while nxt:
                fill(1)
            if main:
                handover(["R2", "M1a", "M1b", "ysb", "xDa", "xDb"])
            stage_end("B", mode)
            if not main:
                return
            setbanks([0, 1, 2, 3, 4, 5, 6])
            for jo in range(16):
                if jo % 2 == 0:
                    wb0 = getw(H_WB + (jo // 2) * 2)
                    wb1 = getw(H_WB + (jo // 2) * 2 + 1)
                    wg = getw(h_in(G_B + jo * 128)[0])
                cb = jo % 2
                b0 = nextbank()
                inproj(wg[0], wg[1], cb, b0)
                act(sgt[:], pb[b0][:], AF.Sigmoid, ["pb%d" % b0], ["sgt"])
                b1 = nextbank()
                for kc in range(32):
                    wb_ = wb0 if kc < 16 else wb1
                    mm(pb[b1][:], wb_[0][:, kc % 16, cb * 128:(cb + 1) * 128], ynT[:, kc, :], kc == 0, kc == 31,
                       [wb_[1], ("yn", kc)], ["pb%d" % b1])
                tt("dve", saz[:], pb[b1][:], sgt[:], ALU.mult, ["pb%d" % b1, "sgt"], ["saz"])
                tt("dve", va[:, jo, :], saz[:], ma[:, jo, :], ALU.add, ["saz", ("ma", jo)], [("va", jo)])
            stage_end("M", mode)
            handover([("yn", k) for k in range(32)] + [("xr", tb) for tb in range(4)])
            for tb in range(4):
                dma("sp", xres[:, tb, :], x_src[t0 + tb * 128:t0 + (tb + 1) * 128, :], [], [("xr", tb)], "xr%d" % tb)
            handover([("ma", j) for j in range(16)] + ["maF", "maS"])
            dma("sp", finw_v, finw_d, [], ["maF"], "finw")
            for hq in range(8):
                wo = getw(H_WO + hq)
                for tb in range(4):
                    b0 = nextbank()
                    for kc in range(KC):
                        mm(pb[b0][:, 0:HS], va[:, kc, tb * 128:(tb + 1) * 128], wo[0][:, kc, :], kc == 0, kc == KC - 1,
                           [wo[1], ("va", kc)], ["pb%d" % b0])
                    tt("dve", xres[:, tb, hq * HS:(hq + 1) * HS], xres[:, tb, hq * HS:(hq + 1) * HS], pb[b0][:, 0:HS], ALU.add,
                       [("xr", tb), "pb%d" % b0], [("xr", tb)])
            toks = []
            for tb in range(4):
                act(sq_v, xres[:, tb, :], AF.Square, [("xr", tb)], ["maS"])
                P.op("dve", lambda e: e.reduce_sum(out=ssc2[:, 0:1], in_=sq_v, axis=AX.X), ["maS"], ["ssc2"], cost=2.4)
                ts("dve", ssc2[:, 1:2], ssc2[:, 0:1], 1.0 / D, NORM_EPS, ALU.mult, ALU.add, ["ssc2"], ["ssc2"])
                act(ssc2[:, 1:2], ssc2[:, 1:2], AF.Sqrt, ["ssc2"], ["ssc2"])
                recip(ssc2[:, 1:2], ["ssc2"], ["ssc2"])
                stt("dve", xres[:, tb, :], xres[:, tb, :], ssc2[:, 1:2], finw_v, ALU.mult, ALU.mult,
                    [("xr", tb), "ssc2", "maF"], [("xr", tb)])
                toks.append(dma("sp", out_ap[t0 + tb * 128:t0 + (tb + 1) * 128, :], xres[:, tb, :], [("xr", tb)], [], "out"))
            handover([("xr", tb) for tb in range(4)] + [("R1", j) for j in range(16)])
            handover([("ma", j) for j in range(16)] + ["maF", "maS"])
            return toks

        def program():
            if not (dbg and dbg.get("nopre")):
                for ti in range(NT):
                    tile(x_prev, ti * T, "pre_last" if ti == NT - 1 else "pre")
                for g in range(8):
                    ts("dve", stT[:, g, :], stT[:, g, :], pc(PC_FLAG), None, ALU.mult, None, [("stT", g)], [("stT", g)])
                ts("dve", ucarry[:].rearrange("p j t -> p (j t)"), ucarry[:].rearrange("p j t -> p (j t)"), pc(PC_FLAG), None,
                   ALU.mult, None, [("ucarry", j) for j in range(16)], [("ucarry", j) for j in range(16)])
                ts("dve", carryb[:].rearrange("p j t -> p (j t)"), carryb[:].rearrange("p j t -> p (j t)"), pc(PC_FLAG), None,
                   ALU.mult, None, [("carryb", j) for j in range(48)], [("carryb", j) for j in range(48)])
            final = []
            if dbg and dbg.get("notile"):
                return final
            for ti in range(NT):
                final += tile(x_main, ti * T, "main", out_d)
            return final

        if not dbg:
            final = program()
            P.emit(final_waits=[final[-1]])
        else:
            tensors = dict(hT=hT, dtt=dtt, da=da, cum=cum, ecum=ecum, te=te2[0], dec=dec2[0], R1=R1, va=va, ma=ma, BT=BT2[0], CT=CT2[0],
                           xTg=xTg2[0], zsg=zsg2[0], stT=stT, R3=R3, upad=upad, U_f=U_f,
                           L_f=L_f, ident_f=ident_f, Arow=Arow, rstd=rstd, ucarry=ucarry, carryb=carryb, dahi=dahi,
                           dalo=dalo, R2=R2, stb=stb)
            try:
                program()
            except _Stop:
                pass
            col = 0
            last = None
            for name, ncols in dbg["dump"]:
                tns = tensors[name]
                flat = tns[:] if len(tns.shape) == 2 else tns[:].rearrange("p a b -> p (a b)")
                last = P.op("pool", lambda e, flat=flat, col=col, ncols=ncols: e.dma_start(out=dbg_d[:, col:col + ncols], in_=flat[:, 0:ncols]),
                            [], list(P.last_w.keys()), dma="dbg")
                col += ncols
            P.emit(final_waits=[last])
    return nc


_NC_CACHE = {}


def make_in_maps(x, norm_w, w_in, conv_a_w, conv_a_b, ln_a_w, ln_a_b, w_a_out, conv_b_w, conv_b_b, dt_bias, a_log,
           d_skip, ssm_norm_w, w_b_out, w_o, final_norm_w):
    f = lambda a: np.ascontiguousarray(np.asarray(a, dtype=np.float32))
    x = f(x)
    colmaj = lambda v: f(v).reshape(-1, 128).T
    pcol = np.zeros((128, NPC), np.float32)
    pcol[:, PC_NORMW:PC_NORMW + 16] = colmaj(norm_w[0])
    caw = f(conv_a_w[0])
    pcol[:, PC_CAW:PC_CAW + 496] = caw.reshape(31, 16, 128).transpose(2, 1, 0).reshape(128, 496)
    pcol[:, PC_CAB:PC_CAB + 16] = colmaj(conv_a_b[0])
    pcol[:, PC_LNW:PC_LNW + 16] = colmaj(ln_a_w[0])
    pcol[:, PC_LNB:PC_LNB + 16] = colmaj(ln_a_b[0])
    cbw = f(conv_b_w[0])
    pcol[:, PC_CBW:PC_CBW + 192] = cbw.reshape(4, 48, 128).transpose(2, 1, 0).reshape(128, 192)
    pcol[:, PC_CBB:PC_CBB + 48] = colmaj(conv_b_b[0])
    pcol[:, PC_SNW:PC_SNW + 32] = colmaj(ssm_norm_w[0])
    prow = np.zeros((128, 192), np.float32)
    prow[:, 0:64] = f(dt_bias[0])[None, :]
    prow[:, 64:128] = f(a_log[0])[None, :]
    prow[:, 128:192] = f(d_skip[0])[None, :]
    finw = np.ascontiguousarray(np.broadcast_to(f(final_norm_w)[None, :], (128, D)))
    wi, wa, wb, wo = f(w_in[0]), f(w_a_out[0]), f(w_b_out[0]), f(w_o[0])
    zeros = np.zeros((NTOK, D), np.float32)
    in_maps = []
    for core in range(8):
        b, half = core // 2, core % 2
        pc_ = pcol.copy()
        pc_[:, PC_FLAG] = float(half)
        in_maps.append({
            "x_main": np.ascontiguousarray(x[b, half * NTOK:(half + 1) * NTOK]),
            "x_prev": zeros if half == 0 else np.ascontiguousarray(x[b, 0:NTOK]),
            "w_in": wi, "w_a_out": wa, "w_b_out": wb, "w_o": wo,
            "pcol": pc_, "prow": prow, "finw": finw,
        })
    return in_maps


def kernel(**inputs):
    in_maps = make_in_maps(**inputs)
    if "nc" not in _NC_CACHE:
        _NC_CACHE["nc"] = build_nc()
    res = run_bass_kernel_spmd(_NC_CACHE["nc"], in_maps, core_ids=list(range(8)))
    out = np.empty((4, 2 * NTOK, D), np.float32)
    for core in range(8):
        b, half = core // 2, core % 2
        out[b, half * NTOK:(half + 1) * NTOK] = res.results[core]["out"]
    return out
```
